# Optimizing a Trainium2 kernel written in Bass

```python
import math
import jax, jax.numpy as jnp
from jax import lax
import numpy as np

D_MODEL = 2048
BATCH = 2
SEQ = 8192
DEPTH = 4

N_EVEN = (DEPTH + 1) // 2
N_ODD = DEPTH // 2

RWKV_WIDTH = D_MODEL // 2
RWKV_HEAD = 64
RWKV_HEADS = RWKV_WIDTH // RWKV_HEAD
DECAY_LORA = max(32, int(round(1.8 * RWKV_WIDTH ** 0.5 / 32)) * 32)
AAA_LORA = max(32, int(round(1.8 * RWKV_WIDTH ** 0.5 / 32)) * 32)
GATE_LORA = max(32, int(round(0.6 * RWKV_WIDTH ** 0.8 / 32)) * 32)
RWKV_COLS = 3 * RWKV_WIDTH + DECAY_LORA + AAA_LORA + GATE_LORA
RWKV_LNX_EPS = 64e-5

DIFF_WIDTH = D_MODEL - RWKV_WIDTH
DIFF_HEAD = 128
DIFF_HALF = DIFF_HEAD // 2
DIFF_HEADS = DIFF_WIDTH // DIFF_HEAD
DIFF_COLS = 3 * DIFF_WIDTH
Q_BLOCK = 128
ROPE_THETA = 10000.0
NEG_INF = -1e30

EVEN_IN = RWKV_COLS + DIFF_COLS

GMLP_WIDTH = D_MODEL
GMLP_CHUNK = 128
GMLP_GROUPS = 16
GMLP_GROUP_DIM = GMLP_WIDTH // GMLP_GROUPS

FFN_HIDDEN = -(-8 * D_MODEL // (3 * 256)) * 256

NORM_EPS = 1e-6
SUBLN_EPS = 1e-5
LN_EPS = 1e-5

kernel_name = "hybrid_rwkv7_diffattn_gmlp_trunk"


def rms_norm(x, g, eps=NORM_EPS):
    xf = x.astype(jnp.float32)
    y = xf * lax.rsqrt(jnp.mean(xf * xf, axis=-1, keepdims=True) + eps)
    return (y * g.astype(jnp.float32)).astype(x.dtype)


def layer_norm(x, g, b, eps=LN_EPS):
    xf = x.astype(jnp.float32)
    mu = jnp.mean(xf, axis=-1, keepdims=True)
    xc = xf - mu
    y = xc * lax.rsqrt(jnp.mean(xc * xc, axis=-1, keepdims=True) + eps)
    return (y * g.astype(jnp.float32) + b.astype(jnp.float32)).astype(x.dtype)


def rope_tables(T):
    inv = ROPE_THETA ** (-jnp.arange(0, DIFF_HALF, 2, dtype=jnp.float32) / DIFF_HALF)
    ang = jnp.arange(T, dtype=jnp.float32)[:, None] * inv[None, :]
    return jnp.cos(ang), jnp.sin(ang)


def apply_rope(x, cos, sin):
    x1, x2 = jnp.split(x, 2, axis=-1)
    c = cos[:, None, :].astype(x.dtype)
    s = sin[:, None, :].astype(x.dtype)
    return jnp.concatenate([x1 * c - x2 * s, x2 * c + x1 * s], axis=-1)


def rwkv7_scan(r, w, k, v, a, b):
    B, T, H, N = r.shape
    xs = tuple(jnp.moveaxis(t, 1, 0) for t in (r, w, k, v, a, b))

    def step(S, inp):
        r_t, w_t, k_t, v_t, a_t, b_t = inp
        sa = jnp.einsum('bhvk,bhk->bhv', S, a_t)
        S = (S * w_t[:, :, None, :] + sa[..., None] * b_t[:, :, None, :]
             + v_t[..., None] * k_t[:, :, None, :])
        return S, jnp.einsum('bhvk,bhk->bhv', S, r_t)

    S0 = jnp.zeros((B, H, N, N), jnp.float32)
    _, y = lax.scan(step, S0, xs)
    return jnp.moveaxis(y, 0, 1)


def rwkv7_time_mix(z, mu, w0, w_dec_up, a0, w_a_up, w_g_up, k_k, k_a, r_k, lnx_w, lnx_b):
    B, T, _ = z.shape
    H, N, C = RWKV_HEADS, RWKV_HEAD, RWKV_WIDTH
    f32 = jnp.float32
    z_prev = jnp.pad(z, ((0, 0), (1, 0), (0, 0)))[:, :-1]
    z = z + (z_prev - z) * mu
    r, k, v, xw, xa, xg = jnp.split(
        z, [C, 2 * C, 3 * C, 3 * C + DECAY_LORA, 3 * C + DECAY_LORA + AAA_LORA], axis=-1)
    w = -jax.nn.softplus(-(w0 + jnp.tanh(xw) @ w_dec_up)) - 0.5
    decay = jnp.exp(-jnp.exp(w.astype(f32)))
    a = jax.nn.sigmoid(a0 + xa @ w_a_up)
    g = jax.nn.sigmoid(xg) @ w_g_up
    kk = (k * k_k).reshape(B, T, H, N).astype(f32)
    kk = kk * lax.rsqrt(jnp.maximum(jnp.sum(kk * kk, axis=-1, keepdims=True), 1e-24))
    k = k * (1.0 + (a - 1.0) * k_a)
    rh = r.reshape(B, T, H, N).astype(f32)
    kh = k.reshape(B, T, H, N).astype(f32)
    vh = v.reshape(B, T, H, N).astype(f32)
    ah = a.reshape(B, T, H, N).astype(f32)
    y = rwkv7_scan(rh, decay.reshape(B, T, H, N), kh, vh, -kk, kk * ah)
    mean = jnp.mean(y, axis=-1, keepdims=True)
    yc = y - mean
    y = yc * lax.rsqrt(jnp.mean(yc * yc, axis=-1, keepdims=True) + RWKV_LNX_EPS)
    y = y * lnx_w.reshape(H, N).astype(f32) + lnx_b.reshape(H, N).astype(f32)
    bonus = jnp.sum(rh * kh * r_k.astype(f32), axis=-1, keepdims=True) * vh
    out = (y + bonus).reshape(B, T, C) * g.astype(f32)
    return out.astype(z.dtype)


def diff_attention(z, cos, sin, lam_q1, lam_k1, lam_q2, lam_k2, subln_g, lambda_init):
    B, T, _ = z.shape
    H = DIFF_HEADS
    f32 = jnp.float32
    q, k, v = jnp.split(z, 3, axis=-1)
    q = apply_rope(q.reshape(B, T, 2 * H, DIFF_HALF), cos, sin).reshape(B, T, H, 2, DIFF_HALF)
    k = apply_rope(k.reshape(B, T, 2 * H, DIFF_HALF), cos, sin).reshape(B, T, H, 2, DIFF_HALF)
    v = v.reshape(B, T, H, DIFF_HEAD)
    lam = (jnp.exp(jnp.sum(lam_q1.astype(f32) * lam_k1.astype(f32)))
           - jnp.exp(jnp.sum(lam_q2.astype(f32) * lam_k2.astype(f32))) + lambda_init)
    nb = T // Q_BLOCK
    q_blocks = jnp.moveaxis(q.reshape(B, nb, Q_BLOCK, H, 2, DIFF_HALF), 1, 0)
    q_pos = jnp.arange(T).reshape(nb, Q_BLOCK)
    k_pos = jnp.arange(T)
    scale = DIFF_HALF ** -0.5

    def attend(blk):
        qb, qp = blk
        s = jnp.einsum('bqhmd,bkhmd->bhmqk', qb, k).astype(f32) * scale
        s = jnp.where(qp[:, None] >= k_pos[None, :], s, NEG_INF)
        p = jax.nn.softmax(s, axis=-1)
        attn = p[:, :, 0] - lam * p[:, :, 1]
        return jnp.einsum('bhqk,bkhd->bqhd', attn.astype(v.dtype), v)

    o = lax.map(attend, (q_blocks, q_pos))
    o = jnp.moveaxis(o, 0, 1).reshape(B, T, H, DIFF_HEAD)
    o = rms_norm(o, subln_g, SUBLN_EPS) * (1.0 - lambda_init)
    return o.reshape(B, T, DIFF_WIDTH).astype(z.dtype)


def chunked_sgu(z, ln_g, ln_b, w_s, b_s):
    B, T, _ = z.shape
    z = jax.nn.gelu(z, approximate=False)
    u, v = jnp.split(z, 2, axis=-1)
    v = layer_norm(v, ln_g, ln_b)
    v = v.reshape(B, T // GMLP_CHUNK, GMLP_CHUNK, GMLP_GROUPS, GMLP_GROUP_DIM)
    causal = jnp.tril(jnp.ones((GMLP_CHUNK, GMLP_CHUNK), w_s.dtype))
    sv = jnp.einsum('gts,bcsgd->bctgd', w_s * causal, v) + b_s.T[:, :, None]
    return u * sv.reshape(B, T, GMLP_WIDTH)


def swiglu(h, w_gate, w_up, w_down):
    return (jax.nn.silu(h @ w_gate) * (h @ w_up)) @ w_down


def setup_inputs(seed: int = 0) -> dict:
    key = jax.random.key(seed)
    ks = iter(jax.random.split(key, 64))
    f32 = jnp.float32

    def nrm(shape, scale):
        return scale * jax.random.normal(next(ks), shape, f32)

    def gain(shape):
        return 1.0 + nrm(shape, 0.02)

    D, C, NE, NO = D_MODEL, RWKV_WIDTH, N_EVEN, N_ODD
    return {
        'x': nrm((BATCH, SEQ, D), 1.0),
        'mix_norm': gain((DEPTH, D)),
        'ffn_norm': gain((DEPTH, D)),
        'ffn_w_gate': nrm((DEPTH, D, FFN_HIDDEN), D ** -0.5),
        'ffn_w_up': nrm((DEPTH, D, FFN_HIDDEN), D ** -0.5),
        'ffn_w_down': nrm((DEPTH, FFN_HIDDEN, D), FFN_HIDDEN ** -0.5),
        'ev_w_in': nrm((NE, D, EVEN_IN), D ** -0.5),
        'ev_mu': jax.random.uniform(next(ks), (NE, RWKV_COLS), f32),
        'ev_w0': jax.random.uniform(next(ks), (NE, C), f32, -3.0, 1.0),
        'ev_w_dec_up': nrm((NE, DECAY_LORA, C), 0.5 * DECAY_LORA ** -0.5),
        'ev_a0': nrm((NE, C), 0.5),
        'ev_w_a_up': nrm((NE, AAA_LORA, C), 0.5 * AAA_LORA ** -0.5),
        'ev_w_g_up': nrm((NE, GATE_LORA, C), GATE_LORA ** -0.5),
        'ev_k_k': 0.85 + nrm((NE, C), 0.05),
        'ev_k_a': 1.0 + nrm((NE, C), 0.05),
        'ev_r_k': nrm((NE, RWKV_HEADS, RWKV_HEAD), 0.1),
        'ev_lnx_w': gain((NE, C)),
        'ev_lnx_b': nrm((NE, C), 0.02),
        'ev_lam_q1': nrm((NE, DIFF_HALF), 0.1),
        'ev_lam_k1': nrm((NE, DIFF_HALF), 0.1),
        'ev_lam_q2': nrm((NE, DIFF_HALF), 0.1),
        'ev_lam_k2': nrm((NE, DIFF_HALF), 0.1),
        'ev_subln_g': gain((NE, DIFF_HEAD)),
        'ev_w_out': nrm((NE, RWKV_WIDTH + DIFF_WIDTH, D), D ** -0.5),
        'od_w_in': nrm((NO, D, 2 * GMLP_WIDTH), D ** -0.5),
        'od_ln_g': gain((NO, GMLP_WIDTH)),
        'od_ln_b': nrm((NO, GMLP_WIDTH), 0.02),
        'od_w_s': nrm((NO, GMLP_GROUPS, GMLP_CHUNK, GMLP_CHUNK), GMLP_CHUNK ** -0.5),
        'od_b_s': 1.0 + nrm((NO, GMLP_GROUPS, GMLP_CHUNK), 0.02),
        'od_w_out': nrm((NO, GMLP_WIDTH, D), GMLP_WIDTH ** -0.5),
        'final_norm': gain((D,)),
    }


def reference(x, mix_norm, ffn_norm, ffn_w_gate, ffn_w_up, ffn_w_down,
              ev_w_in, ev_mu, ev_w0, ev_w_dec_up, ev_a0, ev_w_a_up, ev_w_g_up,
              ev_k_k, ev_k_a, ev_r_k, ev_lnx_w, ev_lnx_b,
              ev_lam_q1, ev_lam_k1, ev_lam_q2, ev_lam_k2, ev_subln_g, ev_w_out,
              od_w_in, od_ln_g, od_ln_b, od_w_s, od_b_s, od_w_out, final_norm):
    T = x.shape[1]
    cos, sin = rope_tables(T)
    for i in range(DEPTH):
        j = i // 2
        h = rms_norm(x, mix_norm[i])
        if i % 2 == 0:
            lambda_init = 0.8 - 0.6 * math.exp(-0.3 * i)
            z = h @ ev_w_in[j]
            y_a = rwkv7_time_mix(z[..., :RWKV_COLS], ev_mu[j], ev_w0[j], ev_w_dec_up[j],
                                 ev_a0[j], ev_w_a_up[j], ev_w_g_up[j], ev_k_k[j], ev_k_a[j],
                                 ev_r_k[j], ev_lnx_w[j], ev_lnx_b[j])
            y_b = diff_attention(z[..., RWKV_COLS:], cos, sin, ev_lam_q1[j], ev_lam_k1[j],
                                 ev_lam_q2[j], ev_lam_k2[j], ev_subln_g[j], lambda_init)
            y = jnp.concatenate([y_a, y_b], axis=-1) @ ev_w_out[j]
        else:
            z = h @ od_w_in[j]
            y = chunked_sgu(z, od_ln_g[j], od_ln_b[j], od_w_s[j], od_b_s[j]) @ od_w_out[j]
        x = x + y.astype(x.dtype)
        h = rms_norm(x, ffn_norm[i])
        x = x + swiglu(h, ffn_w_gate[i], ffn_w_up[i], ffn_w_down[i]).astype(x.dtype)
    return rms_norm(x, final_norm)
```

```python
import math
import os
import numpy as np
import concourse.bass as bass
import concourse.mybir as mybir
from concourse.bass_utils import run_bass_kernel_spmd
from contextlib import ExitStack

F32 = mybir.dt.float32
BF16 = mybir.dt.bfloat16
AF = mybir.ActivationFunctionType
ALU = mybir.AluOpType
AX = mybir.AxisListType

ENGS = ("pe", "act", "dve", "pool", "sp")


class Buf:
    def __init__(self, ap, name=""):
        self.ap = ap
        self.name = name
        self.lw = None
        self.rd = []
        self.dsem = None
        self.is_psum = False

    def __getitem__(self, idx):
        return self.ap[idx]


class Prog:
    def __init__(self, nc):
        self.nc = nc
        self.es = ExitStack()
        self.q = {e: [] for e in ENGS}
        self.sems = {}
        self.semval = {}
        self.waited = {e: {} for e in ENGS}
        for e in ENGS:
            self._newsem("E_" + e)
        self.ndsem = 0
        self.out_tokens = []

    def _newsem(self, key):
        h = self.es.enter_context(self.nc.semaphore(key))
        self.sems[key] = h
        self.semval[key] = 0
        return key

    def sbuf(self, shape, dt, name):
        t = self.es.enter_context(self.nc.sbuf_tensor(name, list(shape), dt))
        return Buf(t[:], name)

    def psum(self, shape, dt, name):
        t = self.es.enter_context(self.nc.psum_tensor(name, list(shape), dt))
        b = Buf(t[:], name)
        b.is_psum = True
        return b

    def view(self, ap, name=""):
        return Buf(ap, name)

    def _need(self, eng, reads, writes):
        deps = {}
        for b in reads:
            if b.lw is not None:
                k, v = b.lw
                deps[k] = max(deps.get(k, 0), v)
            if b.is_psum:
                for (k, v) in b.rd:
                    if k != "E_" + eng:
                        deps[k] = max(deps.get(k, 0), v)
        for b in writes:
            if b.lw is not None:
                k, v = b.lw
                deps[k] = max(deps.get(k, 0), v)
            for (k, v) in b.rd:
                deps[k] = max(deps.get(k, 0), v)
        w = self.waited[eng]
        for k, v in deps.items():
            if w.get(k, 0) >= v:
                continue
            w[k] = v
            self.q[eng].append(("wait", k, v))

    def op(self, eng, fn, reads=(), writes=()):
        self._need(eng, reads, writes)
        key = "E_" + eng
        self.semval[key] += 1
        tok = (key, self.semval[key])
        self.q[eng].append(("op", fn, key, 1))
        for b in writes:
            b.lw = tok
            b.rd = []
        for b in reads:
            if b not in writes:
                b.rd.append(tok)
        return tok

    def dma(self, eng, out_buf, in_buf, out_ap=None, in_ap=None, sem_buf=None, **kw):
        sb = sem_buf if sem_buf is not None else out_buf
        if sb.dsem is None:
            sb.dsem = self._newsem("D%d_%s" % (self.ndsem, sb.name))
            self.ndsem += 1
        self._need(eng, [in_buf], [out_buf])
        key = sb.dsem
        self.semval[key] += 16
        tok = (key, self.semval[key])
        oa = out_ap if out_ap is not None else out_buf.ap
        ia = in_ap if in_ap is not None else in_buf.ap
        self.q[eng].append(("op", lambda e, oa=oa, ia=ia, kw=kw: e.dma_start(out=oa, in_=ia, **kw), key, 16))
        out_buf.lw = tok
        out_buf.rd = []
        in_buf.rd.append(tok)
        return tok

    def finish_wait(self, eng, bufs):
        self._need(eng, bufs, [])

    def build(self):
        nc = self.nc
        with nc.Block() as block:
            def mk(ename):
                def body(e):
                    for item in self.q[ename]:
                        if item[0] == "wait":
                            e.wait_ge(self.sems[item[1]], item[2])
                        elif item[0] == "raw":
                            item[1](e)
                        else:
                            _, fn, key, inc = item
                            fn(e).then_inc(self.sems[key], inc)
                return body
            block.tensor(mk("pe"))
            block.scalar(mk("act"))
            block.vector(mk("dve"))
            block.gpsimd(mk("pool"))
            block.sync(mk("sp"))
        self.es.close()


def _w(name):
    def f(self, eng, R, W, *a, **k):
        return self.op(eng, lambda e: getattr(e, name)(*a, **k), R, W)
    return f


for _n in ("tensor_tensor", "tensor_scalar", "scalar_tensor_tensor", "activation", "matmul", "transpose",
           "tensor_copy", "tensor_tensor_scan", "memset", "reciprocal", "affine_select", "copy", "tensor_reduce", "select", "iota"):
    setattr(Prog, "i_" + _n, _w(_n))

D = 2048
KC = D // 128


def load_vec_cols(P, eng, dram_ap_1d, n, name):
    c = n // 128
    t = P.sbuf([128, c], F32, name)
    P.dma(eng, t, P.view(dram_ap_1d), in_ap=dram_ap_1d.rearrange("(c p) -> p c", p=128), allow_slow_non_contiguous=True)
    return t


class Dense:
    def __init__(self, P, NT, TT=1024):
        self.P = P
        self.NT = NT
        self.TT = min(TT, NT)
        TTs = self.TT
        self.hT = P.sbuf([128, KC, TTs], BF16, "hT")
        self.ones_bf = P.sbuf([128, 128], BF16, "ones_bf")
        P.op("pool", lambda e: e.memset(self.ones_bf[:], 1.0), [], [self.ones_bf])
        self.xst = [P.sbuf([128, KC, 128], F32, "xst%d" % i) for i in range(2)]
        self.sq = [P.sbuf([128, KC, 128], BF16, "sq%d" % i) for i in range(2)]
        self.rstd = [P.sbuf([128, 128], F32, "rstd%d" % i) for i in range(2)]
        self.eps = P.sbuf([128, 1], F32, "eps_c")
        P.op("pool", lambda e: e.memset(self.eps[:], 1e-6), [], [self.eps])
        self.wb = [P.sbuf([128, 4096], BF16, "wb%d" % i) for i in range(4)]
        self.wbi = 0
        self.ps = [P.psum([128, 512], F32, "ps%d" % i) for i in range(8)]
        self.psi = 0
        self.nrm_i = 0

    def next_ps(self):
        b = self.ps[self.psi % 6]
        self.psi += 1
        return b

    def next_wb(self):
        b = self.wb[self.wbi % 4]
        self.wbi += 1
        return b

    def rmsnorm(self, xT_d, t0, gain_t, out_hT=None, ntok=None, out_f32_d=None):
        P = self.P
        out_hT = out_hT or self.hT
        ntok = ntok or self.TT
        psn = self.ps[6 + 0]
        for s in range(ntok // 128):
            i = self.nrm_i % 2
            self.nrm_i += 1
            xs, sq, rstd = self.xst[i], self.sq[i], self.rstd[i]
            src = xT_d.ap[:, t0 + s * 128: t0 + (s + 1) * 128].rearrange("(c p) t -> p c t", p=128)
            P.dma("sp", xs, xT_d, in_ap=src)
            P.op("act", lambda e, xs=xs, sq=sq: e.activation(out=sq[:], in_=xs[:], func=AF.Square), [xs], [sq])
            pv = self.ps[6 + (self.nrm_i % 2)]
            for c in range(KC):
                P.op("pe", lambda e, c=c, sq=sq, pv=pv: e.matmul(pv[:, 0:128], lhsT=self.ones_bf[:], rhs=sq[:, c, :], start=(c == 0), stop=(c == KC - 1)), [sq, self.ones_bf], [pv])
            P.op("act", lambda e, pv=pv, rstd=rstd: e.activation(out=rstd[:], in_=pv[:, 0:128], func=AF.Sqrt, scale=1.0 / D, bias=self.eps[:]), [pv, self.eps], [rstd])
            P.op("dve", lambda e, rstd=rstd: e.reciprocal(out=rstd[:], in_=rstd[:]), [rstd], [rstd])
            if out_f32_d is not None:
                for c in range(KC):
                    P.op("dve", lambda e, c=c, xs=xs, rstd=rstd: e.scalar_tensor_tensor(out=xs[:, c, :], in0=xs[:, c, :], scalar=gain_t[:, c:c + 1], in1=rstd[:], op0=ALU.mult, op1=ALU.mult), [xs, rstd, gain_t], [xs])
                P.dma("sp", out_f32_d, xs, out_ap=out_f32_d.ap[:, t0 + s * 128: t0 + (s + 1) * 128].rearrange("(c p) t -> p c t", p=128), sem_buf=xs)
                continue
            for c in range(KC):
                P.op("dve", lambda e, c=c, xs=xs, rstd=rstd, s=s: e.scalar_tensor_tensor(out=out_hT[:, c, s * 128:(s + 1) * 128], in0=xs[:, c, :], scalar=gain_t[:, c:c + 1], in1=rstd[:], op0=ALU.mult, op1=ALU.mult), [xs, rstd, gain_t], [out_hT])

    def load_w(self, w_d, w_ap, shape3):
        P = self.P
        wb = self.next_wb()
        a, b = shape3
        view = wb.ap[:, 0:a * b].rearrange("p (a b) -> p a b", b=b)
        P.dma("pool", wb, w_d, out_ap=view, in_ap=w_ap)
        return wb, view

    def ffn(self, xT_d, t0, wg_d, wu_d, wd_d, aT, FH, xres):
        P = self.P
        TT = self.TT
        nsub = TT // 512
        HC = FH // 128
        half = (HC + 1) // 2
        CB = 2
        for h0 in range(0, HC, half):
            h1 = min(HC, h0 + half)
            for j0 in range(h0, h1, CB):
                nb = min(CB, h1 - j0)
                gb, gv = self.load_w(wg_d, wg_d.ap[:, j0 * 128:(j0 + nb) * 128].rearrange("(c p) n -> p c n", p=128), (KC, nb * 128))
                ub, uv = self.load_w(wu_d, wu_d.ap[:, j0 * 128:(j0 + nb) * 128].rearrange("(c p) n -> p c n", p=128), (KC, nb * 128))
                for jj in range(nb):
                    j = j0 + jj
                    for s in range(nsub):
                        pg = self.next_ps()
                        pu = self.next_ps()
                        for c in range(KC):
                            P.op("pe", lambda e, c=c, pg=pg, gv=gv, jj=jj, s=s: e.matmul(pg[:], lhsT=gv[:, c, jj * 128:(jj + 1) * 128], rhs=self.hT[:, c, s * 512:(s + 1) * 512], start=(c == 0), stop=(c == KC - 1)), [gb, self.hT], [pg])
                        for c in range(KC):
                            P.op("pe", lambda e, c=c, pu=pu, uv=uv, jj=jj, s=s: e.matmul(pu[:], lhsT=uv[:, c, jj * 128:(jj + 1) * 128], rhs=self.hT[:, c, s * 512:(s + 1) * 512], start=(c == 0), stop=(c == KC - 1)), [ub, self.hT], [pu])
                        sg = self.sgt[self.sgi % 2]
                        self.sgi += 1
                        P.op("act", lambda e, pg=pg, sg=sg: e.activation(out=sg[:], in_=pg[:], func=AF.Silu), [pg], [sg])
                        P.op("dve", lambda e, pu=pu, sg=sg, j=j - h0, s=s: e.tensor_tensor(out=aT[:, j, s * 512:(s + 1) * 512], in0=pu[:], in1=sg[:], op=ALU.mult), [pu, sg], [aT])
            nh = h1 - h0
            for dc in range(KC):
                db, dv = self.load_w(wd_d, wd_d.ap[h0 * 128:h1 * 128, dc * 128:(dc + 1) * 128].rearrange("(c p) n -> p c n", p=128), (nh, 128))
                xr = xres[dc % 2]
                P.dma("sp", xr, xT_d, in_ap=xT_d.ap[dc * 128:(dc + 1) * 128, t0:t0 + TT])
                for s in range(nsub):
                    pd = self.next_ps()
                    for c in range(nh):
                        P.op("pe", lambda e, c=c, pd=pd, dv=dv, s=s, nh=nh: e.matmul(pd[:], lhsT=dv[:, c, :], rhs=aT[:, c, s * 512:(s + 1) * 512], start=(c == 0), stop=(c == nh - 1)), [db, aT], [pd])
                    P.op("dve", lambda e, pd=pd, xr=xr, s=s: e.tensor_tensor(out=xr[:, s * 512:(s + 1) * 512], in0=pd[:], in1=xr[:, s * 512:(s + 1) * 512], op=ALU.add), [pd, xr], [xr])
                P.dma("sp", xT_d, xr, out_ap=xT_d.ap[dc * 128:(dc + 1) * 128, t0:t0 + TT], sem_buf=xr)

    def alloc_ffn(self, FH):
        P = self.P
        HC = FH // 128
        half = (HC + 1) // 2
        self.aT = P.sbuf([128, half, self.TT], BF16, "aT")
        self.sgt = [P.sbuf([128, 512], F32, "sg%d" % i) for i in range(2)]
        self.sgi = 0
        self.xres = [P.sbuf([128, self.TT], F32, "xres%d" % i) for i in range(2)]

C0 = math.exp(-0.5)
NCOL = 1056


def rwkv_phase(P, T, hT_d, w_d, mu_d, pc_d, wdec_d, waup_d, wgup_d, yT_d, stage=99, dbg_d=None, st_in_d=None, st_out_d=None):
    NTL = T // 512
    S = P.sbuf
    W = S([128, KC, NCOL], BF16, "rw_W")
    for c in range(KC):
        P.dma("pool", W, w_d, out_ap=W[:, c, :], in_ap=w_d.ap[c * 128:(c + 1) * 128, :])
    mu = S([128, 9], F32, "rw_mu"); P.dma("sp", mu, mu_d)
    pc = S([128, 2, 8], F32, "rw_pc"); P.dma("sp", pc, pc_d)
    wdec = S([128, 256], F32, "rw_wdec"); P.dma("sp", wdec, wdec_d, out_ap=wdec[0:64, :])
    waup = S([128, 256], F32, "rw_waup"); P.dma("sp", waup, waup_d, out_ap=waup[64:128, :])
    wg0 = S([128, 256], F32, "rw_wg0"); P.dma("sp", wg0, wgup_d, in_ap=wgup_d.ap[0:128, :])
    wg1 = S([32, 256], F32, "rw_wg1"); P.dma("sp", wg1, wgup_d, in_ap=wgup_d.ap[128:160, :])
    eps_lnx = S([128, 1], F32, "rw_eps"); P.i_memset("pool", [], [eps_lnx], eps_lnx[:], 64e-5)
    BD = S([128, 128], F32, "rw_BD")
    P.i_memset("pool", [], [BD], BD[:], 1.0)
    P.i_memset("pool", [], [BD], BD[0:64, 64:128], 0.0)
    P.i_memset("pool", [], [BD], BD[64:128, 0:64], 0.0)
    BDm = S([128, 128], F32, "rw_BDm")
    P.i_tensor_scalar("pool", [BD], [BDm], out=BDm[:], in0=BD[:], scalar1=1.0 / 64, scalar2=None, op0=ALU.mult)
    identf = S([128, 128], F32, "rw_identf")
    P.i_memset("pool", [], [identf], identf[:], 1.0)
    P.i_affine_select("pool", [identf], [identf], out=identf[:], in_=identf[:], pattern=[[-1, 128]], compare_op=ALU.is_equal, fill=0.0, base=0, channel_multiplier=1)
    ident = S([128, 128], BF16, "rw_ident")
    P.i_tensor_copy("pool", [identf], [ident], out=ident[:], in_=identf[:])
    I64 = S([128, 64], F32, "rw_I64")
    P.i_tensor_tensor("pool", [identf], [I64], out=I64[:], in0=identf[:, 0:64], in1=identf[:, 64:128], op=ALU.add)
    MASKU = S([128, 128], F32, "rw_MASKU")
    MASKL = S([128, 64], F32, "rw_MASKL")
    P.i_memset("pool", [], [MASKU], MASKU[:], 1.0)
    P.i_memset("pool", [], [MASKL], MASKL[:], 1.0)
    for hh in (0, 64):
        P.i_affine_select("pool", [MASKU], [MASKU], out=MASKU[hh:hh + 64, 0:64], in_=MASKU[hh:hh + 64, 0:64], pattern=[[1, 64]], compare_op=ALU.is_ge, fill=0.0, base=-1, channel_multiplier=-1)
        P.i_affine_select("pool", [MASKU], [MASKU], out=MASKU[hh:hh + 64, 64:128], in_=MASKU[hh:hh + 64, 64:128], pattern=[[1, 64]], compare_op=ALU.is_ge, fill=0.0, base=0, channel_multiplier=-1)
        P.i_affine_select("pool", [MASKL], [MASKL], out=MASKL[hh:hh + 64, :], in_=MASKL[hh:hh + 64, :], pattern=[[-1, 64]], compare_op=ALU.is_ge, fill=0.0, base=-1, channel_multiplier=1)
    SM = S([128, 512], F32, "rw_SM")
    P.i_memset("pool", [], [SM], SM[:], 1.0)
    P.i_memset("pool", [], [SM], SM[:].rearrange("p (c l) -> p c l", l=64)[:, :, 0:1], 0.0)

    HT = [S([128, KC, 512], BF16, "rw_HT%d" % i) for i in range(2)]
    Z = [S([128, 520], F32, "rw_Z%d" % i) for i in range(9)]
    ZS = [S([128, 512], F32, "rw_ZS%d" % i) for i in range(9)]
    TMPi = [0]
    TMP = [S([128, 512], F32, "rw_TMP%d" % i) for i in range(6)]

    def tmp():
        TMPi[0] += 1
        return TMP[TMPi[0] % 6]
    TW = S([64, 512], F32, "rw_TW")
    SG0 = S([128, 512], F32, "rw_SG0")
    SG1 = S([32, 512], F32, "rw_SG1")
    Gb = [S([128, 512], F32, "rw_G%d" % g) for g in range(2)]
    BON = [S([128, 512], F32, "rw_BON%d" % g) for g in range(2)]
    GATE = [S([128, 512], F32, "rw_GATE%d" % g) for g in range(2)]
    BK = [S([128, 8, 2, 64], BF16, "rw_BK%d" % g) for g in range(2)]
    AR = [S([128, 8, 2, 64], BF16, "rw_AR%d" % g) for g in range(2)]
    BH = [S([128, 8, 64], BF16, "rw_BH%d" % g) for g in range(2)]
    KH = [S([128, 8, 64], BF16, "rw_KH%d" % g) for g in range(2)]
    VB = [S([128, 8, 64], BF16, "rw_VB%d" % g) for g in range(2)]
    TM = [[S([128, 8, 64], BF16, "rw_TM%d_%d" % (g, a)) for a in range(4)] for g in range(2)]
    M1Sa = [[S([128, 4, 128], BF16, "rw_M1Sa%d_%d" % (g, b)) for b in range(2)] for g in range(2)]
    M1Sb = [[S([128, 4, 128], BF16, "rw_M1Sb%d_%d" % (g, b)) for b in range(2)] for g in range(2)]
    TT = [[S([128, 4, 64], BF16, "rw_TT%d_%d" % (g, b)) for b in range(2)] for g in range(2)]
    Xb = [S([128, 4, 64], BF16, "rw_X%d" % i) for i in range(2)]
    XTb = [S([128, 4, 64], BF16, "rw_XT%d" % i) for i in range(2)]
    W1S = [S([128, 64], BF16, "rw_W1S%d" % i) for i in range(2)]
    AHS = [S([128, 64], BF16, "rw_AHS%d" % i) for i in range(2)]
    US = [S([128, 64], BF16, "rw_US%d" % i) for i in range(2)]
    STF = [S([128, 64], F32, "rw_STF%d" % g) for g in range(2)]
    STB = [S([128, 64], BF16, "rw_STB%d" % g) for g in range(2)]
    if st_in_d is None:
        for g in range(2):
            P.i_memset("pool", [], [STF[g]], STF[g][:], 0.0)
            P.i_memset("pool", [], [STB[g]], STB[g][:], 0.0)
        for i in range(9):
            P.i_memset("pool", [], [Z[i]], Z[i][:, 0:8], 0.0)
    else:
        P.i_memset("pool", [], [Z[8]], Z[8][:], 0.0)
        for g in range(2):
            P.dma("sp", STF[g], st_in_d, in_ap=st_in_d.ap[:, g * 64:(g + 1) * 64])
            P.i_tensor_copy("dve", [STF[g]], [STB[g]], out=STB[g][:], in_=STF[g][:])
        CARRY = S([128, 72], F32, "rw_CARRY")
        P.dma("sp", CARRY, st_in_d, in_ap=st_in_d.ap[:, 128:200])
        for i in range(9):
            rows = 32 if i == 8 else 128
            P.i_tensor_copy("dve", [CARRY], [Z[i]], out=Z[i][0:rows, 0:8], in_=CARRY[0:rows, i * 8:(i + 1) * 8])
    YS = S([128, 512], F32, "rw_YS")
    OUTB = [S([128, 512], BF16, "rw_OUT%d" % i) for i in range(2)]

    B = [P.psum([128, 512], F32, "rw_ps%d" % i) for i in range(8)]
    misc_i = [0]

    def misc():
        misc_i[0] += 1
        return B[2 + misc_i[0] % 2]
    B4bf = B[4].ap.bitcast(BF16) if hasattr(B[4].ap, "bitcast") else None

    for ti in range(NTL):
        t0 = ti * 512
        ht = HT[ti % 2]
        P.dma("sp", ht, hT_d, in_ap=hT_d.ap[:, t0:t0 + 512].rearrange("(c p) t -> p c t", p=128))
        for i in range(9):
            rows = 32 if i == 8 else 128
            ps = B[i % 2]
            for c in range(KC):
                P.i_matmul("pe", [W, ht], [ps], ps[0:rows, :], lhsT=W[:, c, i * 128:i * 128 + rows], rhs=ht[:, c, :], start=(c == 0), stop=(c == KC - 1))
            if ti > 0:
                P.i_tensor_copy("dve", [Z[i]], [Z[i]], out=Z[i][0:rows, 0:8], in_=Z[i][0:rows, 512:520])
            P.i_copy("act", [ps], [Z[i]], out=Z[i][0:rows, 8:520], in_=ps[0:rows, :])
            tm = tmp()
            P.i_tensor_tensor("dve", [Z[i]], [tm], out=tm[0:rows, :], in0=Z[i][0:rows, 7:519], in1=Z[i][0:rows, 8:520], op=ALU.subtract)
            P.i_scalar_tensor_tensor("dve", [tm, Z[i], mu], [ZS[i]], out=ZS[i][0:rows, :], in0=tm[0:rows, :], scalar=mu[0:rows, i:i + 1], in1=Z[i][0:rows, 8:520], op0=ALU.mult, op1=ALU.add)
        if stage < 1:
            continue
        P.i_activation("act", [ZS[6]], [TW], out=TW[:], in_=ZS[6][0:64, :], func=AF.Tanh)
        P.i_activation("act", [ZS[7]], [SG0], out=SG0[:], in_=ZS[7][:], func=AF.Sigmoid)
        P.i_activation("act", [ZS[8]], [SG1], out=SG1[:], in_=ZS[8][0:32, :], func=AF.Sigmoid)
        for g in range(2):
            gc = slice(g * 128, (g + 1) * 128)
            r_s, k_s, v_s = ZS[g], ZS[2 + g], ZS[4 + g]
            G = Gb[g]
            pcg = lambda j: pc[:, g, j:j + 1]
            pw = misc()
            P.i_matmul("pe", [wdec, TW], [pw], pw[:], lhsT=wdec[0:64, gc], rhs=TW[0:64, :], start=True, stop=True)
            SIG = tmp()
            P.i_activation("act", [pw, pc], [SIG], out=SIG[:], in_=pw[:], func=AF.Sigmoid, bias=pcg(0))
            pa = misc()
            P.i_matmul("pe", [waup, ZS[6]], [pa], pa[:], lhsT=waup[64:128, gc], rhs=ZS[6][64:128, :], start=True, stop=True)
            A = tmp()
            P.i_activation("act", [pa, pc], [A], out=A[:], in_=pa[:], func=AF.Sigmoid, bias=pcg(1))
            CS = tmp()
            P.i_tensor_tensor_scan("dve", [SM, SIG], [CS], out=CS[:], data0=SM[:], data1=SIG[:], initial=0.0, op0=ALU.mult, op1=ALU.add)
            CSX = tmp()
            P.i_tensor_tensor("dve", [CS, SIG], [CSX], out=CSX[:], in0=CS[:], in1=SIG[:], op=ALU.subtract)
            GI = SIG
            P.i_activation("act", [CS], [G], out=G[:], in_=CS[:], func=AF.Exp, scale=-C0)
            GEX = CSX
            P.i_activation("act", [CSX], [GEX], out=GEX[:], in_=CSX[:], func=AF.Exp, scale=-C0)
            P.i_activation("act", [CS], [GI], out=GI[:], in_=CS[:], func=AF.Exp, scale=C0)
            KKR = tmp()
            P.i_tensor_scalar("dve", [k_s, pc], [KKR], out=KKR[:], in0=k_s[:], scalar1=pcg(2), scalar2=None, op0=ALU.mult)
            SQ = CS
            P.i_activation("act", [KKR], [SQ], out=SQ[:], in_=KKR[:], func=AF.Square)
            pss = misc()
            P.i_matmul("pe", [BD, SQ], [pss], pss[:], lhsT=BD[:], rhs=SQ[:], start=True, stop=True)
            RN = SQ
            P.i_tensor_scalar("dve", [pss], [RN], out=RN[:], in0=pss[:], scalar1=1e-24, scalar2=None, op0=ALU.max)
            P.i_activation("act", [RN], [RN], out=RN[:], in_=RN[:], func=AF.Sqrt)
            P.i_reciprocal("dve", [RN], [RN], out=RN[:], in_=RN[:])
            KK = KKR
            P.i_tensor_tensor("dve", [KKR, RN], [KK], out=KK[:], in0=KKR[:], in1=RN[:], op=ALU.mult)
            K2 = RN
            P.i_tensor_scalar("dve", [A, pc], [K2], out=K2[:], in0=A[:], scalar1=pcg(3), scalar2=pcg(7), op0=ALU.mult, op1=ALU.add)
            P.i_tensor_tensor("dve", [K2, k_s], [K2], out=K2[:], in0=K2[:], in1=k_s[:], op=ALU.mult)
            RK = tmp()
            P.i_scalar_tensor_tensor("dve", [r_s, K2, pc], [RK], out=RK[:], in0=r_s[:], scalar=pcg(4), in1=K2[:], op0=ALU.mult, op1=ALU.mult)
            prk = misc()
            P.i_matmul("pe", [BD, RK], [prk], prk[:], lhsT=BD[:], rhs=RK[:], start=True, stop=True)
            P.i_tensor_tensor("dve", [prk, v_s], [BON[g]], out=BON[g][:], in0=prk[:], in1=v_s[:], op=ALU.mult)
            pg = misc()
            P.i_matmul("pe", [wg0, SG0], [pg], pg[:], lhsT=wg0[:, gc], rhs=SG0[:], start=True, stop=False)
            P.i_matmul("pe", [wg1, SG1], [pg], pg[:], lhsT=wg1[0:32, gc], rhs=SG1[0:32, :], start=False, stop=True)
            P.i_copy("act", [pg], [GATE[g]], out=GATE[g][:], in_=pg[:])
            if stage < 2:
                continue
            KA = RK
            P.i_tensor_tensor("dve", [KK, A], [KA], out=KA[:], in0=KK[:], in1=A[:], op=ALU.mult)
            v3 = lambda ap: ap.rearrange("p (c l) -> p c l", l=64)
            P.i_tensor_tensor("dve", [KA, GI], [BK[g]], out=BK[g][:, :, 0, :], in0=v3(KA[:]), in1=v3(GI[:]), op=ALU.mult)
            P.i_tensor_tensor("dve", [K2, GI], [BK[g]], out=BK[g][:, :, 1, :], in0=v3(K2[:]), in1=v3(GI[:]), op=ALU.mult)
            P.i_scalar_tensor_tensor("dve", [KK, GEX], [AR[g]], out=AR[g][:, :, 0, :], in0=v3(KK[:]), scalar=-1.0, in1=v3(GEX[:]), op0=ALU.mult, op1=ALU.mult)
            P.i_tensor_tensor("dve", [r_s, G], [AR[g]], out=AR[g][:, :, 1, :], in0=v3(r_s[:]), in1=v3(G[:]), op=ALU.mult)
            GLb = v3(G[:])[:, :, 63:64].broadcast_to([128, 8, 64])
            P.i_tensor_tensor("dve", [BK[g], G], [BH[g]], out=BH[g][:], in0=BK[g][:, :, 0, :], in1=GLb, op=ALU.mult)
            P.i_tensor_tensor("dve", [BK[g], G], [KH[g]], out=KH[g][:], in0=BK[g][:, :, 1, :], in1=GLb, op=ALU.mult)
            P.i_copy("act", [v_s], [VB[g]], out=VB[g][:], in_=v3(v_s[:]))
            if stage < 3:
                continue
            srcs = [(AR[g], lambda c: AR[g][:, c, 0, :]), (BH[g], lambda c: BH[g][:, c, :]), (KH[g], lambda c: KH[g][:, c, :]), (VB[g], lambda c: VB[g][:, c, :])]
            for pair in range(2):
                PT = B[4].ap.bitcast(BF16).rearrange("p (a c l) -> p a c l", a=2, c=8)
                for a2 in range(2):
                    sb, fn = srcs[pair * 2 + a2]
                    for c in range(8):
                        src = fn(c)
                        for hh in (0, 64):
                            P.i_transpose("pe", [sb, ident], [B[4]], PT[hh:hh + 64, a2, c, :], src[hh:hh + 64, :], ident[hh:hh + 64, hh:hh + 64])
                for a2 in range(2):
                    dst = TM[g][pair * 2 + a2]
                    P.i_copy("act", [B[4]], [dst], out=dst[:], in_=PT[:, a2, :, :])
            if stage < 4:
                continue
            for b in range(2):
                PM1a = B[5].ap.rearrange("p (c n) -> p c n", c=4)
                PM1b = B[6].ap.rearrange("p (c n) -> p c n", c=4)
                PNT = B[2].ap[:, 0:256].rearrange("p (c n) -> p c n", c=4)
                PX = B[3].ap[:, 0:256].rearrange("p (c n) -> p c n", c=4)
                PXT = B[4].ap[:, 0:256].rearrange("p (c n) -> p c n", c=4)
                PTU = B[4].ap[:, 256:512].rearrange("p (c n) -> p c n", c=4)
                for cc in range(4):
                    c = b * 4 + cc
                    for hh in (0, 64):
                        hs = slice(hh, hh + 64)
                        P.i_matmul("pe", [BK[g], AR[g]], [B[5]], PM1a[hs, cc, :], lhsT=BK[g][hs, c, 0, :], rhs=AR[g][hs, c, :, :].rearrange("p a l -> p (a l)"), start=True, stop=True)
                        P.i_matmul("pe", [BK[g], AR[g]], [B[6]], PM1b[hs, cc, :], lhsT=BK[g][hs, c, 1, :], rhs=AR[g][hs, c, :, :].rearrange("p a l -> p (a l)"), start=True, stop=True)
                        P.i_matmul("pe", [BK[g], AR[g]], [B[2]], PNT[hs, cc, :], lhsT=AR[g][hs, c, 0, :], rhs=BK[g][hs, c, 0, :], start=True, stop=True)
                MUb = MASKU[:].rearrange("p (o n) -> p o n", o=1).broadcast_to([128, 4, 128])
                MLb = MASKL[:].rearrange("p (o n) -> p o n", o=1).broadcast_to([128, 4, 64])
                I64b = I64[:].rearrange("p (o n) -> p o n", o=1).broadcast_to([128, 4, 64])
                sa, sbb, tt = M1Sa[g][b], M1Sb[g][b], TT[g][b]
                P.i_tensor_tensor("dve", [B[5], MASKU], [sa], out=sa[:], in0=PM1a, in1=MUb, op=ALU.mult)
                P.i_tensor_tensor("dve", [B[6], MASKU], [sbb], out=sbb[:], in0=PM1b, in1=MUb, op=ALU.mult)
                X, XT = Xb[0], XTb[0]
                P.i_tensor_tensor("dve", [B[2], MASKL], [XT], out=XT[:], in0=PNT, in1=MLb, op=ALU.mult)
                P.i_tensor_copy("dve", [sa], [X], out=X[:], in_=sa[:, :, 0:64])
                P.i_tensor_tensor("dve", [sa, I64], [tt], out=tt[:], in0=sa[:, :, 0:64], in1=I64b, op=ALU.add)
                for lvl in range(1, 6):
                    Xn, XTn = Xb[lvl % 2], XTb[lvl % 2]
                    for cc in range(4):
                        for hh in (0, 64):
                            hs = slice(hh, hh + 64)
                            if lvl < 5:
                                P.i_matmul("pe", [X, XT], [B[3]], PX[hs, cc, :], lhsT=XT[hs, cc, :], rhs=X[hs, cc, :], start=True, stop=True)
                            P.i_matmul("pe", [X, XT], [B[4]], PXT[hs, cc, :], lhsT=X[hs, cc, :], rhs=XT[hs, cc, :], start=True, stop=True)
                    if lvl < 5:
                        P.i_copy("act", [B[3]], [Xn], out=Xn[:], in_=PX)
                    P.i_tensor_copy("dve", [B[4]], [XTn], out=XTn[:], in_=PXT)
                    for cc in range(4):
                        for hh in (0, 64):
                            hs = slice(hh, hh + 64)
                            P.i_matmul("pe", [XTn, tt], [B[4]], PTU[hs, cc, :], lhsT=XTn[hs, cc, :], rhs=tt[hs, cc, :], start=True, stop=True)
                    P.i_tensor_tensor("dve", [B[4], tt], [tt], out=tt[:], in0=PTU, in1=tt[:], op=ALU.add)
                    X, XT = Xn, XTn
        if stage < 5 or ti < int(os.environ.get('SERIAL_FROM', '0')):
            continue
        for c in range(8):
            for g in range(2):
                b, cc = c // 4, c % 4
                ATM, BHTM, KHTM, VTM = TM[g]
                sa, sbb, tt = M1Sa[g][b], M1Sb[g][b], TT[g][b]
                k = g
                w1s, ahs, us = W1S[k], AHS[k], US[k]
                bW1 = bAH = (B[7] if g == 0 else B[4])
                bU = bS = (B[5] if g == 0 else B[6])
                PW1, PAH = bW1.ap[:, 0:64], bW1.ap[:, 64:128]
                PU, PSn = bU.ap[:, 0:64], bU.ap[:, 64:128]
                PY = B[g]
                for hh in (0, 64):
                    hs = slice(hh, hh + 64)
                    P.i_matmul("pe", [sbb, VTM], [bW1], PW1[hs, :], lhsT=sbb[hs, cc, 0:64], rhs=VTM[hs, c, :], start=True, stop=True)
                    P.i_matmul("pe", [ATM, tt], [bAH], PAH[hs, :], lhsT=ATM[hs, c, :], rhs=tt[hs, cc, :], start=True, stop=True)
                if stage < 5.2:
                    continue
                P.i_tensor_copy("dve", [bW1], [w1s], out=w1s[:], in_=PW1)
                P.i_tensor_copy("dve", [bAH], [ahs], out=ahs[:], in_=PAH)
                for hh in (0, 64):
                    hs = slice(hh, hh + 64)
                    P.i_matmul("pe", [tt, w1s], [bU], PU[hs, :], lhsT=tt[hs, cc, :], rhs=w1s[hs, :], start=True, stop=False)
                    P.i_matmul("pe", [ahs, STB[g]], [bU], PU[hs, :], lhsT=ahs[hs, :], rhs=STB[g][hs, :], start=False, stop=True)
                if stage < 5.4:
                    continue
                P.i_tensor_copy("dve", [bU], [us], out=us[:], in_=PU)
                if stage < 5.5:
                    continue
                for hh in (0, 64):
                    hs = slice(hh, hh + 64)
                    ycol = slice(c * 64, (c + 1) * 64)
                    ysel = int(os.environ.get("YSEL", "7"))
                    ymm = [m_ for m_ in range(3) if ysel & (1 << m_)]
                    yops = [([STB[g], AR[g]], STB[g][hs, :], AR[g][hs, c, 1, :]), ([us, sa], us[hs, :], sa[hs, cc, 64:128]), ([VTM, sbb], VTM[hs, c, :], sbb[hs, cc, 64:128])]
                    for m_ in ymm:
                        P.i_matmul("pe", yops[m_][0], [PY], PY[hs, ycol], lhsT=yops[m_][1], rhs=yops[m_][2], start=(m_ == ymm[0]), stop=(m_ == ymm[-1]))
                if stage < 5.6:
                    continue
                for hh in (0, 64):
                    hs = slice(hh, hh + 64)
                    P.i_matmul("pe", [BHTM, us], [bS], PSn[hs, :], lhsT=BHTM[hs, c, :], rhs=us[hs, :], start=True, stop=False)
                    P.i_matmul("pe", [KHTM, VTM], [bS], PSn[hs, :], lhsT=KHTM[hs, c, :], rhs=VTM[hs, c, :], start=False, stop=True)
                if stage < 5.8:
                    continue
                GLc = Gb[g][:, c * 64 + 63:c * 64 + 64]
                P.i_tensor_scalar("dve", [STF[g], Gb[g]], [STF[g]], out=STF[g][:], in0=STF[g][:], scalar1=GLc, scalar2=None, op0=ALU.mult)
                if stage < 5.85:
                    continue
                P.i_tensor_tensor("dve", [STF[g], bS], [STF[g]], out=STF[g][:], in0=PSn, in1=STF[g][:], op=ALU.add)
                if stage < 5.9:
                    continue
                P.i_tensor_copy("dve", [STF[g]], [STB[g]], out=STB[g][:], in_=STF[g][:])
        if stage < 6:
            continue
        for g in range(2):
            PY = B[g]
            pcg = lambda j: pc[:, g, j:j + 1]
            P.i_copy("act", [PY], [YS], out=YS[:], in_=PY[:])
            pm = misc()
            P.i_matmul("pe", [BDm, YS], [pm], pm[:], lhsT=BDm[:], rhs=YS[:], start=True, stop=True)
            YC = tmp()
            P.i_tensor_tensor("dve", [YS, pm], [YC], out=YC[:], in0=YS[:], in1=pm[:], op=ALU.subtract)
            SQ = tmp()
            P.i_activation("act", [YC], [SQ], out=SQ[:], in_=YC[:], func=AF.Square)
            pv = misc()
            P.i_matmul("pe", [BDm, SQ], [pv], pv[:], lhsT=BDm[:], rhs=SQ[:], start=True, stop=True)
            RS = SQ
            P.i_activation("act", [pv, eps_lnx], [RS], out=RS[:], in_=pv[:], func=AF.Sqrt, bias=eps_lnx[:])
            P.i_reciprocal("dve", [RS], [RS], out=RS[:], in_=RS[:])
            P.i_tensor_tensor("dve", [YC, RS], [YC], out=YC[:], in0=YC[:], in1=RS[:], op=ALU.mult)
            P.i_tensor_scalar("dve", [YC, pc], [YC], out=YC[:], in0=YC[:], scalar1=pcg(5), scalar2=pcg(6), op0=ALU.mult, op1=ALU.add)
            P.i_tensor_tensor("dve", [YC, BON[g]], [YC], out=YC[:], in0=YC[:], in1=BON[g][:], op=ALU.add)
            ob = OUTB[g]
            P.i_tensor_tensor("dve", [YC, GATE[g]], [ob], out=ob[:], in0=YC[:], in1=GATE[g][:], op=ALU.mult)
            P.dma("sp", yT_d, ob, out_ap=yT_d.ap[g * 128:(g + 1) * 128, t0:t0 + 512], sem_buf=ob)

    if dbg_d is not None:
        for g in range(2):
            P.dma("sp", dbg_d, STF[g], out_ap=dbg_d.ap[:, g, 0:64], sem_buf=STF[g])
            P.dma("sp", dbg_d, Gb[g], out_ap=dbg_d.ap[:, g, 64:576], sem_buf=Gb[g])
            P.dma("sp", dbg_d, BON[g], out_ap=dbg_d.ap[:, g, 576:1088], sem_buf=BON[g])
        P.finish_wait("sp", [dbg_d])

    if st_out_d is not None:
        for g in range(2):
            P.dma("sp", st_out_d, STF[g], out_ap=st_out_d.ap[:, g * 64:(g + 1) * 64], sem_buf=STF[g])
        CARRYO = S([128, 72], F32, "rw_CARRYO")
        for i in range(9):
            P.i_tensor_copy("dve", [Z[i]], [CARRYO], out=CARRYO[:, i * 8:(i + 1) * 8], in_=Z[i][:, 512:520])
        P.dma("sp", st_out_d, CARRYO, out_ap=st_out_d.ap[:, 128:200], sem_buf=CARRYO)
        P.finish_wait("sp", [st_out_d])


def attn_phase(P, T, hT_d, w_d, cos_d, sin_d, rt_d, lamv_d, sg_d, yT_d, li_d):
    S = P.sbuf
    NTL = T // 512
    NB = T // 128
    W = S([128, KC, 768], BF16, "at_W")
    for c in range(KC):
        P.dma("pool", W, w_d, out_ap=W[:, c, :], in_ap=w_d.ap[c * 128:(c + 1) * 128, :])
    RT = S([128, 128], F32, "at_RT"); P.dma("sp", RT, rt_d)
    lamv = S([128, 4, 64], F32, "at_lamv"); P.dma("sp", lamv, lamv_d)
    SG = S([128, 128], F32, "at_SG"); P.dma("sp", SG, sg_d)
    li = S([128, 2], F32, "at_li"); P.dma("sp", li, li_d)
    P.i_tensor_scalar("dve", [SG, li], [SG], out=SG[:], in0=SG[:], scalar1=li[:, 1:2], scalar2=None, op0=ALU.mult)
    lp = S([128, 2, 64], F32, "at_lp")
    P.i_tensor_tensor("dve", [lamv], [lp], out=lp[:, 0, :], in0=lamv[:, 0, :], in1=lamv[:, 1, :], op=ALU.mult)
    P.i_tensor_tensor("dve", [lamv], [lp], out=lp[:, 1, :], in0=lamv[:, 2, :], in1=lamv[:, 3, :], op=ALU.mult)
    ls = S([128, 2], F32, "at_ls")
    P.i_tensor_reduce("dve", [lp], [ls], out=ls[:], in_=lp[:], axis=AX.X, op=ALU.add)
    P.i_activation("act", [ls], [ls], out=ls[:], in_=ls[:], func=AF.Exp)
    nlam = S([128, 1], F32, "at_nlam")
    P.i_tensor_tensor("dve", [ls], [nlam], out=nlam[:], in0=ls[:, 1:2], in1=ls[:, 0:1], op=ALU.subtract)
    P.i_tensor_scalar("dve", [nlam, li], [nlam], out=nlam[:], in0=nlam[:], scalar1=li[:, 0:1], scalar2=None, op0=ALU.add)
    eps = S([128, 1], F32, "at_eps"); P.i_memset("pool", [], [eps], eps[:], 1e-5)
    identf = S([128, 128], F32, "at_identf")
    P.i_memset("pool", [], [identf], identf[:], 1.0)
    P.i_affine_select("pool", [identf], [identf], out=identf[:], in_=identf[:], pattern=[[-1, 128]], compare_op=ALU.is_equal, fill=0.0, base=0, channel_multiplier=1)
    ident = S([128, 128], BF16, "at_ident")
    P.i_tensor_copy("pool", [identf], [ident], out=ident[:], in_=identf[:])
    CM = S([128, 4, 512], BF16, "at_CM")
    P.i_memset("pool", [], [CM], CM[:], 1.0)
    for j in range(4):
        P.i_affine_select("pool", [CM], [CM], out=CM[:, j, :], in_=CM[:, j, :], pattern=[[1, 512]], compare_op=ALU.is_ge, fill=0.0, base=-128 * j, channel_multiplier=-1)
    KT = S([128, 2, T], BF16, "at_KT")
    VT = S([128, NB, 2, 132], BF16, "at_VT")
    P.i_memset("pool", [], [VT], VT[:, :, :, 128:129], 1.0)
    HT = [S([128, KC, 512], BF16, "at_HT%d" % i) for i in range(2)]
    QT = [S([128, 2, 512], BF16, "at_QT%d" % i) for i in range(2)]
    XF = [S([128, 512], F32, "at_XF%d" % i) for i in range(2)]
    XC = [S([128, 512], F32, "at_XC%d" % i) for i in range(2)]
    COS = [S([128, 512], F32, "at_COS%d" % i) for i in range(2)]
    SIN = [S([128, 512], F32, "at_SIN%d" % i) for i in range(2)]
    PTs = [S([128, 512], BF16, "at_PT%d" % i) for i in range(3)]
    O1 = S([128, 4, 128], F32, "at_O1")
    O2 = S([128, 4, 128], F32, "at_O2")
    OSQ = S([128, 4, 128], F32, "at_OSQ")
    RD = S([128, 4], F32, "at_RD")
    SSQ = S([128, 4], F32, "at_SSQ")
    YB = S([128, 4, 128], BF16, "at_YB")
    YTB = [S([128, 512], BF16, "at_YTB%d" % i) for i in range(2)]
    B = [P.psum([128, 512], F32, "at_ps%d" % i) for i in range(7)]
    cnt = {"x": 0, "s": 0, "p": 0}

    for ti in range(NTL):
        t0 = ti * 512
        ht = HT[ti % 2]
        P.dma("sp", ht, hT_d, in_ap=hT_d.ap[:, t0:t0 + 512].rearrange("(c p) t -> p c t", p=128))
        cs_, sn_ = COS[ti % 2], SIN[ti % 2]
        P.dma("sp", cs_, cos_d, in_ap=cos_d.ap[:, t0:t0 + 512])
        P.dma("sp", sn_, sin_d, in_ap=sin_d.ap[:, t0:t0 + 512])
        qt = QT[ti % 2]
        for i in range(4):
            ps = B[cnt["x"] % 2]; xf = XF[cnt["x"] % 2]; xc = XC[cnt["x"] % 2]; cnt["x"] += 1
            for c in range(KC):
                P.i_matmul("pe", [W, ht], [ps], ps[:], lhsT=W[:, c, i * 128:(i + 1) * 128], rhs=ht[:, c, :], start=(c == 0), stop=(c == KC - 1))
            P.i_copy("act", [ps], [xf], out=xf[:], in_=ps[:])
            P.i_matmul("pe", [RT, xf], [ps], ps[:], lhsT=RT[:], rhs=xf[:], start=True, stop=True)
            P.i_tensor_tensor("dve", [xf, cs_], [xc], out=xc[:], in0=xf[:], in1=cs_[:], op=ALU.mult)
            P.i_tensor_tensor("dve", [ps, sn_], [xf], out=xf[:], in0=ps[:], in1=sn_[:], op=ALU.mult)
            dst = qt[:, i, :] if i < 2 else KT[:, i - 2, t0:t0 + 512]
            P.i_tensor_tensor("dve", [xf, xc], [qt if i < 2 else KT], out=dst, in0=xf[:], in1=xc[:], op=ALU.add)
        for sb in range(4):
            ps = B[cnt["x"] % 2]; cnt["x"] += 1
            for c in range(KC):
                P.i_matmul("pe", [W, ht], [ps], ps[:, 0:256], lhsT=ht[:, c, sb * 128:(sb + 1) * 128], rhs=W[:, c, 512:768], start=(c == 0), stop=(c == KC - 1))
            P.i_copy("act", [ps], [VT], out=VT[:, ti * 4 + sb, :, 0:128], in_=ps[:, 0:256].rearrange("p (h d) -> p h d", h=2))
        nkb = 4 * (ti + 1)
        for h in range(2):
            for m in range(2):
                ms = slice(m * 64, (m + 1) * 64)
                ACC = [B[4].ap[:, 0:258].rearrange("p (a d) -> p a d", a=2), B[5].ap[:, 0:258].rearrange("p (a d) -> p a d", a=2)]
                for kb in range(nkb):
                    ps = B[2 + cnt["s"] % 2]; cnt["s"] += 1
                    P.i_matmul("pe", [KT, qt], [ps], ps[:], lhsT=KT[ms, h, kb * 128:(kb + 1) * 128], rhs=qt[ms, h, :], start=True, stop=True)
                    pt = PTs[cnt["p"] % 3]; cnt["p"] += 1
                    P.i_activation("act", [ps], [pt], out=pt[:], in_=ps[:], func=AF.Exp, scale=0.125)
                    j = kb - 4 * ti
                    if j >= 0:
                        P.i_tensor_tensor("dve", [pt, CM], [pt], out=pt[:], in0=pt[:], in1=CM[:, j, :], op=ALU.mult)
                    for qs in range(4):
                        if j > qs:
                            continue
                        bank = qs // 2
                        first = (kb == 0 and qs % 2 == 0)
                        last = (kb == min(nkb - 1, 4 * ti + qs))
                        P.i_matmul("pe", [pt, VT], [B[4 + bank]], ACC[bank][:, qs % 2, :], lhsT=pt[:, qs * 128:(qs + 1) * 128], rhs=VT[:, kb, h, 0:129], start=first, stop=last, skip_group_check=True)
                for bank in range(2):
                    qsl = slice(bank * 2, bank * 2 + 2)
                    P.i_reciprocal("dve", [B[4 + bank]], [RD], out=RD[:, qsl], in_=ACC[bank][:, :, 128])
                    dstO = O1 if m == 0 else O2
                    P.i_tensor_tensor("dve", [B[4 + bank], RD], [dstO], out=dstO[:, qsl, :], in0=ACC[bank][:, :, 0:128], in1=RD[:, qsl].rearrange("p (a o) -> p a o", o=1).broadcast_to([128, 2, 128]), op=ALU.mult)
            P.i_scalar_tensor_tensor("dve", [O2, nlam, O1], [O1], out=O1[:], in0=O2[:], scalar=nlam[:], in1=O1[:], op0=ALU.mult, op1=ALU.add)
            P.i_activation("act", [O1], [OSQ], out=OSQ[:], in_=O1[:], func=AF.Square)
            P.i_tensor_reduce("dve", [OSQ], [SSQ], out=SSQ[:], in_=OSQ[:], axis=AX.X, op=ALU.add)
            P.i_activation("act", [SSQ, eps], [SSQ], out=SSQ[:], in_=SSQ[:], func=AF.Sqrt, scale=1.0 / 128, bias=eps[:])
            P.i_reciprocal("dve", [SSQ], [SSQ], out=SSQ[:], in_=SSQ[:])
            P.i_tensor_tensor("dve", [O1, SSQ], [O1], out=O1[:], in0=O1[:], in1=SSQ[:].rearrange("p (a o) -> p a o", o=1).broadcast_to([128, 4, 128]), op=ALU.mult)
            P.i_tensor_tensor("dve", [O1, SG], [YB], out=YB[:], in0=O1[:], in1=SG[:].rearrange("p (o d) -> p o d", o=1).broadcast_to([128, 4, 128]), op=ALU.mult)
            PTR = B[6].ap.bitcast(BF16)[:, 0:512]
            for qs in range(4):
                P.i_transpose("pe", [YB, ident], [B[6]], PTR[:, qs * 128:(qs + 1) * 128], YB[:, qs, :], ident[:])
            yb = YTB[h]
            P.i_copy("act", [B[6]], [yb], out=yb[:], in_=PTR)
            P.dma("sp", yT_d, yb, out_ap=yT_d.ap[h * 128:(h + 1) * 128, t0:t0 + 512], sem_buf=yb)
import ml_dtypes
NTC = 2048
FH = 5632
SEQ = 8192
RSEG = 8192


def proj_residual(dn, xT_d, t0, ntok, w_d, kch, rhs, rhs_buf):
    P = dn.P
    for dc in range(KC):
        db, dv = dn.load_w(w_d, w_d.ap[:, dc * 128:(dc + 1) * 128].rearrange("(c p) n -> p c n", p=128), (kch, 128))
        xr = dn.xres[dc % 2]
        P.dma("sp", xr, xT_d, out_ap=xr[:, 0:ntok], in_ap=xT_d.ap[dc * 128:(dc + 1) * 128, t0:t0 + ntok])
        for s in range(ntok // 512):
            pd = dn.next_ps()
            for c in range(kch):
                P.i_matmul("pe", [db, rhs_buf], [pd], pd[:], lhsT=dv[:, c, :], rhs=rhs[:, c, s * 512:(s + 1) * 512], start=(c == 0), stop=(c == kch - 1))
            P.i_tensor_tensor("dve", [pd, xr], [xr], out=xr[:, s * 512:(s + 1) * 512], in0=pd[:], in1=xr[:, s * 512:(s + 1) * 512], op=ALU.add)
        P.dma("sp", xT_d, xr, out_ap=xT_d.ap[dc * 128:(dc + 1) * 128, t0:t0 + ntok], in_ap=xr[:, 0:ntok], sem_buf=xr)


class GMLP:
    def __init__(self, dn):
        P = dn.P
        self.dn = dn
        self.vtm = P.sbuf([128, 4, 2048], BF16, "gm_vtm")
        self.gt = [P.sbuf([128, 256], F32, "gm_gt%d" % i) for i in range(2)]
        self.gti = 0
        self.ST = P.sbuf([128, 4, 8, 6], F32, "gm_ST")
        self.MV = P.sbuf([128, 4, 2], F32, "gm_MV")
        self.RS = P.sbuf([128, 4], F32, "gm_RS")
        self.eps = P.sbuf([128, 1], F32, "gm_eps")
        P.i_memset("pool", [], [self.eps], self.eps[:], 1e-5)
        self.wsT = P.sbuf([128, 16, 128], BF16, "gm_wsT")
        self.T2 = P.sbuf([128, 16, 128], F32, "gm_T2")
        self.tmp = [P.sbuf([128, 4, 128], F32, "gm_tmp%d" % i) for i in range(2)]
        self.tmpi = 0
        self.lng = P.sbuf([128, 16], F32, "gm_lng")
        self.lnb = P.sbuf([128, 16], F32, "gm_lnb")

    def setup(self, wsT_d, bsb_d, lng_d, lnb_d):
        dn, P = self.dn, self.dn.P
        P.dma("sp", self.lng, lng_d)
        P.dma("sp", self.lnb, lnb_d)
        P.dma("sp", self.T2, bsb_d)
        stg = dn.xst[0]
        sv = stg.ap[:, :, :]
        P.dma("sp", stg, wsT_d, out_ap=sv)
        P.i_affine_select("pool", [stg], [stg], out=sv, in_=sv, pattern=[[0, 16], [1, 128]], compare_op=ALU.is_ge, fill=0.0, base=0, channel_multiplier=-1)
        P.i_tensor_copy("pool", [stg], [self.wsT], out=self.wsT[:], in_=sv)
        for g4 in range(4):
            ps = dn.next_ps()
            for gg in range(4):
                g = g4 * 4 + gg
                P.i_matmul("pe", [dn.ones_bf, self.wsT], [ps], ps[:, gg * 128:(gg + 1) * 128], lhsT=dn.ones_bf[:], rhs=self.wsT[:, g, :], start=True, stop=True)
            for gg in range(4):
                g = g4 * 4 + gg
                P.i_scalar_tensor_tensor("dve", [ps, self.lnb, self.T2], [self.T2], out=self.T2[:, g, :], in0=ps[:, gg * 128:(gg + 1) * 128], scalar=self.lnb[:, g:g + 1], in1=self.T2[:, g, :], op0=ALU.mult, op1=ALU.add)

    def run(self, xT_d, t0, win_d, wout_d):
        dn, P = self.dn, self.dn.P
        hT, aT = dn.hT, dn.aT
        uT = aT.ap[:, 0:16, 0:512]
        vtm = self.vtm
        for fb in range(0, 16, 2):
            wb_, wv = dn.load_w(win_d, win_d.ap[:, fb * 128:(fb + 2) * 128].rearrange("(c p) n -> p c n", p=128), (KC, 256))
            for jj in range(2):
                ps = dn.next_ps()
                for c in range(KC):
                    P.i_matmul("pe", [wb_, hT], [ps], ps[:], lhsT=wv[:, c, jj * 128:(jj + 1) * 128], rhs=hT[:, c, 0:512], start=(c == 0), stop=(c == KC - 1))
                P.i_activation("act", [ps], [aT], out=uT[:, fb + jj, :], in_=ps[:], func=AF.Gelu)
        for cb in range(8):
            wb_, wv = dn.load_w(win_d, win_d.ap[:, 2048 + cb * 256:2048 + (cb + 1) * 256].rearrange("(c p) n -> p c n", p=128), (KC, 256))
            for tc in range(4):
                ps = dn.next_ps()
                for c in range(KC):
                    P.i_matmul("pe", [wb_, hT], [ps], ps[:, 0:256], lhsT=hT[:, c, tc * 128:(tc + 1) * 128], rhs=wv[:, c, :], start=(c == 0), stop=(c == KC - 1))
                gt = self.gt[self.gti % 2]
                self.gti += 1
                P.i_activation("act", [ps], [gt], out=gt[:], in_=ps[:, 0:256], func=AF.Gelu)
                P.op("dve", lambda e, gt=gt, tc=tc, cb=cb: e.bn_stats(out=self.ST[:, tc, cb, :], in_=gt[:]), [gt], [self.ST])
                P.i_tensor_copy("pool", [gt], [vtm], out=vtm[:, tc, cb * 256:(cb + 1) * 256], in_=gt[:])
        for tc in range(4):
            P.op("dve", lambda e, tc=tc: e.bn_aggr(out=self.MV[:, tc, :], in_=self.ST[:, tc, :, :].rearrange("p a b -> p (a b)")), [self.ST], [self.MV])
        P.i_activation("act", [self.MV, self.eps], [self.RS], out=self.RS[:], in_=self.MV[:, :, 1], func=AF.Sqrt, bias=self.eps[:])
        P.i_reciprocal("dve", [self.RS], [self.RS], out=self.RS[:], in_=self.RS[:])
        for tc in range(4):
            P.i_tensor_scalar("dve", [vtm, self.MV, self.RS], [vtm], out=vtm[:, tc, :], in0=vtm[:, tc, :], scalar1=self.MV[:, tc, 0:1], scalar2=self.RS[:, tc:tc + 1], op0=ALU.subtract, op1=ALU.mult)
        for tc in range(4):
            for g4 in range(4):
                ps = dn.next_ps()
                for gg in range(4):
                    g = g4 * 4 + gg
                    P.i_matmul("pe", [vtm, self.wsT], [ps], ps[:, gg * 128:(gg + 1) * 128], lhsT=vtm[:, tc, g * 128:(g + 1) * 128], rhs=self.wsT[:, g, :], start=True, stop=True)
                tmp = self.tmp[self.tmpi % 2]
                self.tmpi += 1
                for gg in range(4):
                    g = g4 * 4 + gg
                    P.i_scalar_tensor_tensor("dve", [ps, self.lng, self.T2], [tmp], out=tmp[:, gg, :], in0=ps[:, gg * 128:(gg + 1) * 128], scalar=self.lng[:, g:g + 1], in1=self.T2[:, g, :], op0=ALU.mult, op1=ALU.add)
                usl = uT[:, g4 * 4:(g4 + 1) * 4, tc * 128:(tc + 1) * 128]
                P.i_tensor_tensor("dve", [tmp, aT], [aT], out=usl, in0=usl, in1=tmp[:], op=ALU.mult)
        proj_residual(dn, xT_d, t0, 512, wout_d, 16, uT, aT)


def _inp(nc, n, s, d=F32):
    return nc.dram_tensor(n, list(s), d, kind="ExternalInput").ap()


def build_e1():
    nc = bass.Bass("TRN2", target_bir_lowering=False)
    x = _inp(nc, "xT", [D, NTC]); gn = _inp(nc, "gn", [128, KC])
    h = nc.dram_tensor("hT_out", [D, NTC], BF16, kind="ExternalOutput").ap()
    P = Prog(nc)
    xd, hd = P.view(x), P.view(h)
    dn = Dense(P, NTC)
    g_t = P.sbuf([128, KC], F32, "g_t"); P.dma("sp", g_t, P.view(gn))
    for t0 in range(0, NTC, dn.TT):
        dn.rmsnorm(xd, t0, g_t)
        P.dma("sp", hd, dn.hT, out_ap=h[:, t0:t0 + dn.TT].rearrange("(c p) t -> p c t", p=128), sem_buf=dn.hT)
    P.finish_wait("sp", [hd])
    P.build()
    return nc


def build_rwkv(T):
    nc = bass.Bass("TRN2", target_bir_lowering=False)
    hT = _inp(nc, "hT_in", [2048, T], BF16)
    w = _inp(nc, "w", [2048, NCOL]); mu = _inp(nc, "mu", [128, 9]); pc = _inp(nc, "pc", [128, 2, 8])
    wdec = _inp(nc, "wdec", [64, 256]); waup = _inp(nc, "waup", [64, 256]); wgup = _inp(nc, "wgup", [160, 256])
    use_state = (T != SEQ)
    if use_state:
        st_in = _inp(nc, "st_in", [128, 200])
    ya = nc.dram_tensor("yaT", [256, T], BF16, kind="ExternalOutput").ap()
    if use_state:
        st_out = nc.dram_tensor("st_out", [128, 200], F32, kind="ExternalOutput").ap()
    P = Prog(nc)
    yad = P.view(ya)
    rwkv_phase(P, T, P.view(hT), P.view(w), P.view(mu), P.view(pc), P.view(wdec), P.view(waup), P.view(wgup), yad,
               st_in_d=(P.view(st_in) if use_state else None), st_out_d=(P.view(st_out) if use_state else None))
    P.finish_wait("sp", [yad])
    P.build()
    return nc


def build_attn(T):
    nc = bass.Bass("TRN2", target_bir_lowering=False)
    hT = _inp(nc, "hT_in", [2048, T], BF16)
    wq = _inp(nc, "wq", [2048, 768]); cos = _inp(nc, "cos", [128, T]); sin = _inp(nc, "sin", [128, T]); rt = _inp(nc, "rt", [128, 128])
    lamv = _inp(nc, "lamv", [128, 4, 64]); sg = _inp(nc, "sg", [128, 128]); li = _inp(nc, "li", [128, 2])
    yb = nc.dram_tensor("ybT", [256, T], BF16, kind="ExternalOutput").ap()
    P = Prog(nc)
    ybd = P.view(yb)
    attn_phase(P, T, P.view(hT), P.view(wq), P.view(cos), P.view(sin), P.view(rt), P.view(lamv), P.view(sg), ybd, P.view(li))
    P.finish_wait("sp", [ybd])
    P.build()
    return nc


def build_e3(last):
    nc = bass.Bass("TRN2", target_bir_lowering=False)
    x = _inp(nc, "xT", [D, NTC]); yT = _inp(nc, "yT", [D, NTC], BF16); wo = _inp(nc, "wo", [D, D])
    gains = _inp(nc, "gains", [128, 4, KC])
    f = [dict(wg=_inp(nc, "wg%d" % k, [D, FH]), wu=_inp(nc, "wu%d" % k, [D, FH]), wd=_inp(nc, "wd%d" % k, [FH, D])) for k in range(2)]
    win = _inp(nc, "win", [D, 4096]); wout = _inp(nc, "wout", [D, D])
    wsT = _inp(nc, "wsT", [128, 16, 128]); bsb = _inp(nc, "bsb", [128, 16, 128]); lng = _inp(nc, "lng", [128, 16]); lnb = _inp(nc, "lnb", [128, 16])
    xo = nc.dram_tensor("xT_out", [D, NTC], F32, kind="ExternalOutput").ap()
    if last:
        fo = nc.dram_tensor("fin", [D, NTC], F32, kind="ExternalOutput").ap()
    else:
        fo = nc.dram_tensor("hT_out", [D, NTC], BF16, kind="ExternalOutput").ap()
    P = Prog(nc)
    xd, xod, fod, yTd = P.view(x), P.view(xo), P.view(fo), P.view(yT)
    dn = Dense(P, NTC); dn.alloc_ffn(FH)
    gm = GMLP(dn)
    g_ts = []
    for k in range(4):
        gt_ = P.sbuf([128, KC], F32, "g_t%d" % k)
        P.dma("sp", gt_, P.view(gains), in_ap=gains[:, k, :])
        g_ts.append(gt_)
    for c in range(KC):
        P.dma("sp", xod, xd, out_ap=xo[c * 128:(c + 1) * 128, :], in_ap=x[c * 128:(c + 1) * 128, :], sem_buf=dn.xres[0])
    TT = dn.TT
    for t0 in range(0, NTC, TT):
        P.dma("sp", dn.aT, yTd, out_ap=dn.aT.ap[:, 0:16, :], in_ap=yT[:, t0:t0 + TT].rearrange("(c p) t -> p c t", p=128))
        proj_residual(dn, xod, t0, TT, P.view(wo), 16, dn.aT.ap[:, 0:16, :], dn.aT)
    for t0 in range(0, NTC, TT):
        dn.rmsnorm(xod, t0, g_ts[0])
        dn.ffn(xod, t0, P.view(f[0]["wg"]), P.view(f[0]["wu"]), P.view(f[0]["wd"]), dn.aT, FH, dn.xres)
    gm.setup(P.view(wsT), P.view(bsb), P.view(lng), P.view(lnb))
    for t0 in range(0, NTC, 512):
        dn.rmsnorm(xod, t0, g_ts[1], ntok=512)
        gm.run(xod, t0, P.view(win), P.view(wout))
    for t0 in range(0, NTC, TT):
        dn.rmsnorm(xod, t0, g_ts[2])
        dn.ffn(xod, t0, P.view(f[1]["wg"]), P.view(f[1]["wu"]), P.view(f[1]["wd"]), dn.aT, FH, dn.xres)
    for t0 in range(0, NTC, TT):
        if last:
            dn.rmsnorm(xod, t0, g_ts[3], out_f32_d=fod)
        else:
            dn.rmsnorm(xod, t0, g_ts[3])
            P.dma("sp", fod, dn.hT, out_ap=fo[:, t0:t0 + TT].rearrange("(c p) t -> p c t", p=128), sem_buf=dn.hT)
    P.finish_wait("sp", [xod, fod])
    P.build()
    return nc


def _pcol(v):
    return np.ascontiguousarray(np.asarray(v, np.float32).reshape(-1, 128).T)


def _rope_consts(T):
    inv = (10000.0 ** (-np.arange(0, 64, 2, dtype=np.float32) / 64)).astype(np.float32)
    ang = np.arange(T, dtype=np.float32)[:, None] * inv[None, :]
    c, s = np.cos(ang).astype(np.float32), np.sin(ang).astype(np.float32)
    idx = np.arange(128) % 32
    cosT = np.ascontiguousarray(c[:, idx].T)
    sinT = np.ascontiguousarray(s[:, idx].T)
    RT = np.zeros((128, 128), np.float32)
    for po in range(128):
        if po % 64 < 32:
            RT[po + 32, po] = -1.0
        else:
            RT[po - 32, po] = 1.0
    return cosT, sinT, RT


_PROGS = {}


def _prog(key, fn):
    if key not in _PROGS:
        _PROGS[key] = fn()
    return _PROGS[key]


def _run(nc, in_maps):
    res = run_bass_kernel_spmd(nc, in_maps, core_ids=list(range(8)))
    return res.results


def kernel(**inp):
    inp = {k: np.asarray(v) for k, v in inp.items()}
    x = inp["x"]
    B, T, Dm = x.shape
    cores = [(c // 4, c % 4) for c in range(8)]
    C = 1024
    cosT, sinT, RT = _rope_consts(T)

    def even_layer(i, hT_c):
        j = i // 2
        li = 0.8 - 0.6 * math.exp(-0.3 * i)
        hT_b = [np.ascontiguousarray(np.concatenate([hT_c[b * 4 + q] for q in range(4)], axis=1)) for b in range(B)]
        Wfull = inp["ev_w_in"][j]
        maps = []
        for (b, g) in cores:
            chs = np.arange(g * 256, g * 256 + 256)
            cols = np.concatenate([chs, C + chs, 2 * C + chs, np.arange(3 * C, 3 * C + 288)])
            mu_c = inp["ev_mu"][j][cols]
            mu_t = np.zeros((128, 9), np.float32)
            for k in range(8):
                mu_t[:, k] = mu_c[k * 128:(k + 1) * 128]
            mu_t[:32, 8] = mu_c[1024:1056]
            per = lambda v: v.reshape(-1)[chs].reshape(2, 128).T
            pcv = np.zeros((128, 2, 8), np.float32)
            for idx, nm in enumerate(["ev_w0", "ev_a0", "ev_k_k", "ev_k_a", "ev_r_k", "ev_lnx_w", "ev_lnx_b"]):
                pcv[:, :, idx] = per(inp[nm][j])
            pcv[:, :, 7] = np.float32(1.0) - pcv[:, :, 3]
            maps.append({"w": np.ascontiguousarray(Wfull[:, cols]), "mu": mu_t, "pc": pcv,
                         "wdec": np.ascontiguousarray(inp["ev_w_dec_up"][j][:, chs]), "waup": np.ascontiguousarray(inp["ev_w_a_up"][j][:, chs]),
                         "wgup": np.ascontiguousarray(inp["ev_w_g_up"][j][:, chs])})
        state = [np.zeros((128, 200), np.float32) for _ in range(8)]
        segs = [[] for _ in range(8)]
        for k in range(T // RSEG):
            mk_ = [dict(maps[c], hT_in=np.ascontiguousarray(hT_b[cores[c][0]][:, k * RSEG:(k + 1) * RSEG])) for c in range(8)]
            if RSEG != T:
                for c in range(8):
                    mk_[c]["st_in"] = state[c]
            rr = _run(_prog(("rwkv", RSEG), lambda: build_rwkv(RSEG)), mk_)
            for c in range(8):
                if RSEG != T:
                    state[c] = rr[c]["st_out"]
                segs[c].append(rr[c]["yaT"])
        ra = [{"yaT": np.concatenate(segs[c], axis=1)} for c in range(8)]
        lamv = np.ascontiguousarray(np.broadcast_to(np.stack([inp["ev_lam_q1"][j], inp["ev_lam_k1"][j], inp["ev_lam_q2"][j], inp["ev_lam_k2"][j]])[None], (128, 4, 64))).astype(np.float32)
        sg = np.ascontiguousarray(np.broadcast_to(inp["ev_subln_g"][j][None], (128, 128))).astype(np.float32)
        liv = np.ascontiguousarray(np.broadcast_to(np.array([-li, 1.0 - li], np.float32)[None], (128, 2)))
        maps = []
        for (b, g) in cores:
            hc = np.arange(g * 256, g * 256 + 256)
            cols = 3360 + np.concatenate([hc, 1024 + hc, 2048 + hc])
            maps.append({"hT_in": hT_b[b], "wq": np.ascontiguousarray(Wfull[:, cols]), "cos": cosT, "sin": sinT, "rt": RT, "lamv": lamv, "sg": sg, "li": liv})
        rb = _run(_prog(("attn", T), lambda: build_attn(T)), maps)
        yT_c = []
        for c, (b, q) in enumerate(cores):
            ts = slice(q * NTC, (q + 1) * NTC)
            parts = [ra[b * 4 + g]["yaT"][:, ts] for g in range(4)] + [rb[b * 4 + g]["ybT"][:, ts] for g in range(4)]
            yT_c.append(np.ascontiguousarray(np.concatenate(parts, axis=0)))
        return yT_c

    def e3(i, xT_c, yT_c, last):
        j = i // 2
        jo = (i + 1) // 2
        nxt = inp["final_norm"] if last else inp["mix_norm"][i + 2]
        gains = np.ascontiguousarray(np.stack([_pcol(inp["ffn_norm"][i]), _pcol(inp["mix_norm"][i + 1]), _pcol(inp["ffn_norm"][i + 1]), _pcol(nxt)], axis=1))
        wsT = np.ascontiguousarray(inp["od_w_s"][jo].transpose(2, 0, 1))
        bsb = np.ascontiguousarray(np.broadcast_to(inp["od_b_s"][jo][None], (128, 16, 128))).astype(np.float32)
        shared = {"wo": inp["ev_w_out"][j], "gains": gains,
                  "wg0": inp["ffn_w_gate"][i], "wu0": inp["ffn_w_up"][i], "wd0": inp["ffn_w_down"][i],
                  "wg1": inp["ffn_w_gate"][i + 1], "wu1": inp["ffn_w_up"][i + 1], "wd1": inp["ffn_w_down"][i + 1],
                  "win": inp["od_w_in"][jo], "wout": inp["od_w_out"][jo], "wsT": wsT, "bsb": bsb,
                  "lng": _pcol(inp["od_ln_g"][jo]), "lnb": _pcol(inp["od_ln_b"][jo])}
        maps = [dict(shared, xT=xT_c[c], yT=yT_c[c]) for c in range(8)]
        return _run(_prog(("e3", last), lambda: build_e3(last)), maps)

    xT_c = [np.ascontiguousarray(x[b, q * NTC:(q + 1) * NTC, :].T) for (b, q) in cores]
    g0 = _pcol(inp["mix_norm"][0])
    r = _run(_prog("e1", build_e1), [{"xT": xT_c[c], "gn": g0} for c in range(8)])
    hT_c = [r[c]["hT_out"] for c in range(8)]
    yT_c = even_layer(0, hT_c)
    r = e3(0, xT_c, yT_c, False)
    xT_c = [r[c]["xT_out"] for c in range(8)]
    hT_c = [r[c]["hT_out"] for c in range(8)]
    yT_c = even_layer(2, hT_c)
    r = e3(2, xT_c, yT_c, True)
    out = np.empty((B, T, Dm), np.float32)
    for c, (b, q) in enumerate(cores):
        out[b, q * NTC:(q + 1) * NTC, :] = r[c]["fin"].T
    return out
```

```python
import math
import os
import numpy as np
import concourse.bass as bass
import concourse.mybir as mybir
from concourse.bass_utils import run_bass_kernel_spmd
from contextlib import ExitStack

F32 = mybir.dt.float32
BF16 = mybir.dt.bfloat16
AF = mybir.ActivationFunctionType
ALU = mybir.AluOpType
AX = mybir.AxisListType

ENGS = ("pe", "act", "dve", "pool", "sp")


class Buf:
    def __init__(self, ap, name=""):
        self.ap = ap
        self.name = name
        self.lw = None
        self.rd = []
        self.dsem = None
        self.is_psum = False

    def __getitem__(self, idx):
        return self.ap[idx]


class Prog:
    def __init__(self, nc):
        self.nc = nc
        self.es = ExitStack()
        self.q = {e: [] for e in ENGS}
        self.sems = {}
        self.semval = {}
        self.waited = {e: {} for e in ENGS}
        for e in ENGS:
            self._newsem("E_" + e)
        self.ndsem = 0
        self.out_tokens = []

    def _newsem(self, key):
        h = self.es.enter_context(self.nc.semaphore(key))
        self.sems[key] = h
        self.semval[key] = 0
        return key

    def sbuf(self, shape, dt, name):
        t = self.es.enter_context(self.nc.sbuf_tensor(name, list(shape), dt))
        return Buf(t[:], name)

    def psum(self, shape, dt, name):
        t = self.es.enter_context(self.nc.psum_tensor(name, list(shape), dt))
        b = Buf(t[:], name)
        b.is_psum = True
        return b

    def view(self, ap, name=""):
        return Buf(ap, name)

    def _need(self, eng, reads, writes):
        deps = {}
        for b in reads:
            if b.lw is not None:
                k, v = b.lw
                deps[k] = max(deps.get(k, 0), v)
            if b.is_psum:
                for (k, v) in b.rd:
                    if k != "E_" + eng:
                        deps[k] = max(deps.get(k, 0), v)
        for b in writes:
            if b.lw is not None:
                k, v = b.lw
                deps[k] = max(deps.get(k, 0), v)
            for (k, v) in b.rd:
                deps[k] = max(deps.get(k, 0), v)
        w = self.waited[eng]
        for k, v in deps.items():
            if w.get(k, 0) >= v:
                continue
            if eng == "pe" and k == "E_pe":
                continue
            w[k] = v
            self.q[eng].append(("wait", k, v))

    def op(self, eng, fn, reads=(), writes=()):
        self._need(eng, reads, writes)
        key = "E_" + eng
        self.semval[key] += 1
        tok = (key, self.semval[key])
        self.q[eng].append(("op", fn, key, 1))
        for b in writes:
            b.lw = tok
            b.rd = []
        for b in reads:
            if b not in writes:
                b.rd.append(tok)
        return tok

    def dma(self, eng, out_buf, in_buf, out_ap=None, in_ap=None, sem_buf=None, **kw):
        sb = sem_buf if sem_buf is not None else out_buf
        if sb.dsem is None:
            sb.dsem = self._newsem("D%d_%s" % (self.ndsem, sb.name))
            self.ndsem += 1
        self._need(eng, [in_buf], [out_buf])
        key = sb.dsem
        self.semval[key] += 16
        tok = (key, self.semval[key])
        oa = out_ap if out_ap is not None else out_buf.ap
        ia = in_ap if in_ap is not None else in_buf.ap
        self.q[eng].append(("op", lambda e, oa=oa, ia=ia, kw=kw: e.dma_start(out=oa, in_=ia, **kw), key, 16))
        out_buf.lw = tok
        out_buf.rd = []
        in_buf.rd.append(tok)
        return tok

    def finish_wait(self, eng, bufs):
        self._need(eng, bufs, [])

    def build(self):
        nc = self.nc
        with nc.Block() as block:
            def mk(ename):
                def body(e):
                    for item in self.q[ename]:
                        if item[0] == "wait":
                            e.wait_ge(self.sems[item[1]], item[2])
                        elif item[0] == "raw":
                            item[1](e)
                        else:
                            _, fn, key, inc = item
                            fn(e).then_inc(self.sems[key], inc)
                return body
            block.tensor(mk("pe"))
            block.scalar(mk("act"))
            block.vector(mk("dve"))
            block.gpsimd(mk("pool"))
            block.sync(mk("sp"))
        self.es.close()


def _w(name):
    def f(self, eng, R, W, *a, **k):
        return self.op(eng, lambda e: getattr(e, name)(*a, **k), R, W)
    return f


for _n in ("tensor_tensor", "tensor_scalar", "scalar_tensor_tensor", "activation", "matmul", "transpose",
           "tensor_copy", "tensor_tensor_scan", "memset", "reciprocal", "affine_select", "copy", "tensor_reduce", "select", "iota"):
    setattr(Prog, "i_" + _n, _w(_n))

D = 2048
KC = D // 128


def load_vec_cols(P, eng, dram_ap_1d, n, name):
    c = n // 128
    t = P.sbuf([128, c], F32, name)
    P.dma(eng, t, P.view(dram_ap_1d), in_ap=dram_ap_1d.rearrange("(c p) -> p c", p=128), allow_slow_non_contiguous=True)
    return t


class Dense:
    def __init__(self, P, NT, TT=1024):
        self.P = P
        self.NT = NT
        self.TT = min(TT, NT)
        TTs = self.TT
        self.hT = P.sbuf([128, KC, TTs], BF16, "hT")
        self.ones_bf = P.sbuf([128, 128], BF16, "ones_bf")
        P.op("pool", lambda e: e.memset(self.ones_bf[:], 1.0), [], [self.ones_bf])
        self.xst = [P.sbuf([128, KC, 128], F32, "xst%d" % i) for i in range(2)]
        self.sq = [P.sbuf([128, KC, 128], BF16, "sq%d" % i) for i in range(2)]
        self.rstd = [P.sbuf([128, 128], F32, "rstd%d" % i) for i in range(2)]
        self.eps = P.sbuf([128, 1], F32, "eps_c")
        P.op("pool", lambda e: e.memset(self.eps[:], 1e-6), [], [self.eps])
        self.wb = [P.sbuf([128, 4096], BF16, "wb%d" % i) for i in range(4)]
        self.wbi = 0
        self.ps = [P.psum([128, 512], F32, "ps%d" % i) for i in range(8)]
        self.psi = 0
        self.nrm_i = 0

    def next_ps(self):
        b = self.ps[self.psi % 6]
        self.psi += 1
        return b

    def next_wb(self):
        b = self.wb[self.wbi % 4]
        self.wbi += 1
        return b

    def rmsnorm(self, xT_d, t0, gain_t, out_hT=None, ntok=None, out_f32_d=None):
        P = self.P
        out_hT = out_hT or self.hT
        ntok = ntok or self.TT
        psn = self.ps[6 + 0]
        for s in range(ntok // 128):
            i = self.nrm_i % 2
            self.nrm_i += 1
            xs, sq, rstd = self.xst[i], self.sq[i], self.rstd[i]
            src = xT_d.ap[:, t0 + s * 128: t0 + (s + 1) * 128].rearrange("(c p) t -> p c t", p=128)
            P.dma("sp", xs, xT_d, in_ap=src)
            P.op("act", lambda e, xs=xs, sq=sq: e.activation(out=sq[:], in_=xs[:], func=AF.Square), [xs], [sq])
            pv = self.ps[6 + (self.nrm_i % 2)]
            for c in range(KC):
                P.op("pe", lambda e, c=c, sq=sq, pv=pv: e.matmul(pv[:, 0:128], lhsT=self.ones_bf[:], rhs=sq[:, c, :], start=(c == 0), stop=(c == KC - 1)), [sq, self.ones_bf], [pv])
            P.op("act", lambda e, pv=pv, rstd=rstd: e.activation(out=rstd[:], in_=pv[:, 0:128], func=AF.Sqrt, scale=1.0 / D, bias=self.eps[:]), [pv, self.eps], [rstd])
            P.op("dve", lambda e, rstd=rstd: e.reciprocal(out=rstd[:], in_=rstd[:]), [rstd], [rstd])
            if out_f32_d is not None:
                for c in range(KC):
                    P.op("dve", lambda e, c=c, xs=xs, rstd=rstd: e.scalar_tensor_tensor(out=xs[:, c, :], in0=xs[:, c, :], scalar=gain_t[:, c:c + 1], in1=rstd[:], op0=ALU.mult, op1=ALU.mult), [xs, rstd, gain_t], [xs])
                P.dma("sp", out_f32_d, xs, out_ap=out_f32_d.ap[:, t0 + s * 128: t0 + (s + 1) * 128].rearrange("(c p) t -> p c t", p=128), sem_buf=xs)
                continue
            for c in range(KC):
                P.op("dve", lambda e, c=c, xs=xs, rstd=rstd, s=s: e.scalar_tensor_tensor(out=out_hT[:, c, s * 128:(s + 1) * 128], in0=xs[:, c, :], scalar=gain_t[:, c:c + 1], in1=rstd[:], op0=ALU.mult, op1=ALU.mult), [xs, rstd, gain_t], [out_hT])

    def load_w(self, w_d, w_ap, shape3):
        P = self.P
        wb = self.next_wb()
        a, b = shape3
        view = wb.ap[:, 0:a * b].rearrange("p (a b) -> p a b", b=b)
        P.dma("pool", wb, w_d, out_ap=view, in_ap=w_ap)
        return wb, view

    def ffn(self, xT_d, t0, wg_d, wu_d, wd_d, aT, FH, xres):
        P = self.P
        TT = self.TT
        nsub = TT // 512
        HC = FH // 128
        half = (HC + 1) // 2
        CB = 2
        for h0 in range(0, HC, half):
            h1 = min(HC, h0 + half)
            for j0 in range(h0, h1, CB):
                nb = min(CB, h1 - j0)
                gb, gv = self.load_w(wg_d, wg_d.ap[:, j0 * 128:(j0 + nb) * 128].rearrange("(c p) n -> p c n", p=128), (KC, nb * 128))
                ub, uv = self.load_w(wu_d, wu_d.ap[:, j0 * 128:(j0 + nb) * 128].rearrange("(c p) n -> p c n", p=128), (KC, nb * 128))
                for jj in range(nb):
                    j = j0 + jj
                    for s in range(nsub):
                        pg = self.next_ps()
                        pu = self.next_ps()
                        for c in range(KC):
                            P.op("pe", lambda e, c=c, pg=pg, gv=gv, jj=jj, s=s: e.matmul(pg[:], lhsT=gv[:, c, jj * 128:(jj + 1) * 128], rhs=self.hT[:, c, s * 512:(s + 1) * 512], start=(c == 0), stop=(c == KC - 1)), [gb, self.hT], [pg])
                        for c in range(KC):
                            P.op("pe", lambda e, c=c, pu=pu, uv=uv, jj=jj, s=s: e.matmul(pu[:], lhsT=uv[:, c, jj * 128:(jj + 1) * 128], rhs=self.hT[:, c, s * 512:(s + 1) * 512], start=(c == 0), stop=(c == KC - 1)), [ub, self.hT], [pu])
                        sg = self.sgt[self.sgi % 2]
                        self.sgi += 1
                        P.op("act", lambda e, pg=pg, sg=sg: e.activation(out=sg[:], in_=pg[:], func=AF.Silu), [pg], [sg])
                        P.op("dve", lambda e, pu=pu, sg=sg, j=j - h0, s=s: e.tensor_tensor(out=aT[:, j, s * 512:(s + 1) * 512], in0=pu[:], in1=sg[:], op=ALU.mult), [pu, sg], [aT])
            nh = h1 - h0
            for dc in range(KC):
                db, dv = self.load_w(wd_d, wd_d.ap[h0 * 128:h1 * 128, dc * 128:(dc + 1) * 128].rearrange("(c p) n -> p c n", p=128), (nh, 128))
                xr = xres[dc % 2]
                P.dma("sp", xr, xT_d, in_ap=xT_d.ap[dc * 128:(dc + 1) * 128, t0:t0 + TT])
                for s in range(nsub):
                    pd = self.next_ps()
                    for c in range(nh):
                        P.op("pe", lambda e, c=c, pd=pd, dv=dv, s=s, nh=nh: e.matmul(pd[:], lhsT=dv[:, c, :], rhs=aT[:, c, s * 512:(s + 1) * 512], start=(c == 0), stop=(c == nh - 1)), [db, aT], [pd])
                    P.op("dve", lambda e, pd=pd, xr=xr, s=s: e.tensor_tensor(out=xr[:, s * 512:(s + 1) * 512], in0=pd[:], in1=xr[:, s * 512:(s + 1) * 512], op=ALU.add), [pd, xr], [xr])
                P.dma("sp", xT_d, xr, out_ap=xT_d.ap[dc * 128:(dc + 1) * 128, t0:t0 + TT], sem_buf=xr)

    def alloc_ffn(self, FH):
        P = self.P
        HC = FH // 128
        half = (HC + 1) // 2
        self.aT = P.sbuf([128, half, self.TT], BF16, "aT")
        self.sgt = [P.sbuf([128, 512], F32, "sg%d" % i) for i in range(2)]
        self.sgi = 0
        self.xres = [P.sbuf([128, self.TT], F32, "xres%d" % i) for i in range(2)]

C0 = math.exp(-0.5)
NCOL = 1056


def rwkv_phase(P, T, hT_d, w_d, mu_d, pc_d, wdec_d, waup_d, wgup_d, yT_d, stage=99, dbg_d=None, st_in_d=None, st_out_d=None):
    NTL = T // 512
    S = P.sbuf
    W = S([128, KC, NCOL], BF16, "rw_W")
    for c in range(KC):
        P.dma("pool", W, w_d, out_ap=W[:, c, :], in_ap=w_d.ap[c * 128:(c + 1) * 128, :])
    mu = S([128, 9], F32, "rw_mu"); P.dma("sp", mu, mu_d)
    pc = S([128, 2, 8], F32, "rw_pc"); P.dma("sp", pc, pc_d)
    wdec = S([128, 256], F32, "rw_wdec"); P.dma("sp", wdec, wdec_d, out_ap=wdec[0:64, :])
    waup = S([128, 256], F32, "rw_waup"); P.dma("sp", waup, waup_d, out_ap=waup[64:128, :])
    wg0 = S([128, 256], F32, "rw_wg0"); P.dma("sp", wg0, wgup_d, in_ap=wgup_d.ap[0:128, :])
    wg1 = S([32, 256], F32, "rw_wg1"); P.dma("sp", wg1, wgup_d, in_ap=wgup_d.ap[128:160, :])
    eps_lnx = S([128, 1], F32, "rw_eps"); P.i_memset("pool", [], [eps_lnx], eps_lnx[:], 64e-5)
    BD = S([128, 128], F32, "rw_BD")
    P.i_memset("pool", [], [BD], BD[:], 1.0)
    P.i_memset("pool", [], [BD], BD[0:64, 64:128], 0.0)
    P.i_memset("pool", [], [BD], BD[64:128, 0:64], 0.0)
    BDm = S([128, 128], F32, "rw_BDm")
    P.i_tensor_scalar("pool", [BD], [BDm], out=BDm[:], in0=BD[:], scalar1=1.0 / 64, scalar2=None, op0=ALU.mult)
    identf = S([128, 128], F32, "rw_identf")
    P.i_memset("pool", [], [identf], identf[:], 1.0)
    P.i_affine_select("pool", [identf], [identf], out=identf[:], in_=identf[:], pattern=[[-1, 128]], compare_op=ALU.is_equal, fill=0.0, base=0, channel_multiplier=1)
    ident = S([128, 128], BF16, "rw_ident")
    P.i_tensor_copy("pool", [identf], [ident], out=ident[:], in_=identf[:])
    I64 = S([128, 64], F32, "rw_I64")
    P.i_tensor_tensor("pool", [identf], [I64], out=I64[:], in0=identf[:, 0:64], in1=identf[:, 64:128], op=ALU.add)
    MASKU = S([128, 128], F32, "rw_MASKU")
    MASKL = S([128, 64], F32, "rw_MASKL")
    P.i_memset("pool", [], [MASKU], MASKU[:], 1.0)
    P.i_memset("pool", [], [MASKL], MASKL[:], 1.0)
    for hh in (0, 64):
        P.i_affine_select("pool", [MASKU], [MASKU], out=MASKU[hh:hh + 64, 0:64], in_=MASKU[hh:hh + 64, 0:64], pattern=[[1, 64]], compare_op=ALU.is_ge, fill=0.0, base=-1, channel_multiplier=-1)
        P.i_affine_select("pool", [MASKU], [MASKU], out=MASKU[hh:hh + 64, 64:128], in_=MASKU[hh:hh + 64, 64:128], pattern=[[1, 64]], compare_op=ALU.is_ge, fill=0.0, base=0, channel_multiplier=-1)
        P.i_affine_select("pool", [MASKL], [MASKL], out=MASKL[hh:hh + 64, :], in_=MASKL[hh:hh + 64, :], pattern=[[-1, 64]], compare_op=ALU.is_ge, fill=0.0, base=-1, channel_multiplier=1)
    SM = S([128, 512], F32, "rw_SM")
    P.i_memset("pool", [], [SM], SM[:], 1.0)
    P.i_memset("pool", [], [SM], SM[:].rearrange("p (c l) -> p c l", l=64)[:, :, 0:1], 0.0)

    HT = [S([128, KC, 512], BF16, "rw_HT%d" % i) for i in range(2)]
    Z = [S([128, 520], F32, "rw_Z%d" % i) for i in range(9)]
    ZS = [S([128, 512], F32, "rw_ZS%d" % i) for i in range(9)]
    TMPi = [0]
    TMP = [S([128, 512], F32, "rw_TMP%d" % i) for i in range(6)]

    def tmp():
        TMPi[0] += 1
        return TMP[TMPi[0] % 6]
    TW = S([64, 512], F32, "rw_TW")
    SG0 = S([128, 512], F32, "rw_SG0")
    SG1 = S([32, 512], F32, "rw_SG1")
    Gb = [S([128, 512], F32, "rw_G%d" % g) for g in range(2)]
    BON = [S([128, 512], F32, "rw_BON%d" % g) for g in range(2)]
    GATE = [S([128, 512], F32, "rw_GATE%d" % g) for g in range(2)]
    BK = [S([128, 8, 2, 64], BF16, "rw_BK%d" % g) for g in range(2)]
    AR = [S([128, 8, 2, 64], BF16, "rw_AR%d" % g) for g in range(2)]
    BH = [S([128, 8, 64], BF16, "rw_BH%d" % g) for g in range(2)]
    KH = [S([128, 8, 64], BF16, "rw_KH%d" % g) for g in range(2)]
    VB = [S([128, 8, 64], BF16, "rw_VB%d" % g) for g in range(2)]
    TM = [[S([128, 8, 64], BF16, "rw_TM%d_%d" % (g, a)) for a in range(4)] for g in range(2)]
    M1Sa = [[S([128, 4, 128], BF16, "rw_M1Sa%d_%d" % (g, b)) for b in range(2)] for g in range(2)]
    M1Sb = [[S([128, 4, 128], BF16, "rw_M1Sb%d_%d" % (g, b)) for b in range(2)] for g in range(2)]
    TT = [[S([128, 4, 64], BF16, "rw_TT%d_%d" % (g, b)) for b in range(2)] for g in range(2)]
    Xb = [S([128, 4, 64], BF16, "rw_X%d" % i) for i in range(2)]
    XTb = [S([128, 4, 64], BF16, "rw_XT%d" % i) for i in range(2)]
    W1S = [S([128, 64], BF16, "rw_W1S%d" % i) for i in range(2)]
    AHS = [S([128, 64], BF16, "rw_AHS%d" % i) for i in range(2)]
    US = [S([128, 64], BF16, "rw_US%d" % i) for i in range(2)]
    STF = [S([128, 64], F32, "rw_STF%d" % g) for g in range(2)]
    STB = [S([128, 64], BF16, "rw_STB%d" % g) for g in range(2)]
    if st_in_d is None:
        for g in range(2):
            P.i_memset("pool", [], [STF[g]], STF[g][:], 0.0)
            P.i_memset("pool", [], [STB[g]], STB[g][:], 0.0)
        for i in range(9):
            P.i_memset("pool", [], [Z[i]], Z[i][:, 0:8], 0.0)
    else:
        P.i_memset("pool", [], [Z[8]], Z[8][:], 0.0)
        for g in range(2):
            P.dma("sp", STF[g], st_in_d, in_ap=st_in_d.ap[:, g * 64:(g + 1) * 64])
            P.i_tensor_copy("dve", [STF[g]], [STB[g]], out=STB[g][:], in_=STF[g][:])
        CARRY = S([128, 72], F32, "rw_CARRY")
        P.dma("sp", CARRY, st_in_d, in_ap=st_in_d.ap[:, 128:200])
        for i in range(9):
            rows = 32 if i == 8 else 128
            P.i_tensor_copy("dve", [CARRY], [Z[i]], out=Z[i][0:rows, 0:8], in_=CARRY[0:rows, i * 8:(i + 1) * 8])
    YS = S([128, 512], F32, "rw_YS")
    OUTB = [S([128, 512], BF16, "rw_OUT%d" % i) for i in range(2)]

    B = [P.psum([128, 512], F32, "rw_ps%d" % i) for i in range(8)]
    misc_i = [0]

    def misc():
        misc_i[0] += 1
        return B[2 + misc_i[0] % 2]
    B4bf = B[4].ap.bitcast(BF16) if hasattr(B[4].ap, "bitcast") else None

    for ti in range(NTL):
        t0 = ti * 512
        ht = HT[ti % 2]
        P.dma("sp", ht, hT_d, in_ap=hT_d.ap[:, t0:t0 + 512].rearrange("(c p) t -> p c t", p=128))
        for i in range(9):
            rows = 32 if i == 8 else 128
            ps = B[i % 2]
            for c in range(KC):
                P.i_matmul("pe", [W, ht], [ps], ps[0:rows, :], lhsT=W[:, c, i * 128:i * 128 + rows], rhs=ht[:, c, :], start=(c == 0), stop=(c == KC - 1))
            if ti > 0:
                P.i_tensor_copy("dve", [Z[i]], [Z[i]], out=Z[i][0:rows, 0:8], in_=Z[i][0:rows, 512:520])
            P.i_copy("act", [ps], [Z[i]], out=Z[i][0:rows, 8:520], in_=ps[0:rows, :])
            tm = tmp()
            P.i_tensor_tensor("dve", [Z[i]], [tm], out=tm[0:rows, :], in0=Z[i][0:rows, 7:519], in1=Z[i][0:rows, 8:520], op=ALU.subtract)
            P.i_scalar_tensor_tensor("dve", [tm, Z[i], mu], [ZS[i]], out=ZS[i][0:rows, :], in0=tm[0:rows, :], scalar=mu[0:rows, i:i + 1], in1=Z[i][0:rows, 8:520], op0=ALU.mult, op1=ALU.add)
        if stage < 1:
            continue
        P.i_activation("act", [ZS[6]], [TW], out=TW[:], in_=ZS[6][0:64, :], func=AF.Tanh)
        P.i_activation("act", [ZS[7]], [SG0], out=SG0[:], in_=ZS[7][:], func=AF.Sigmoid)
        P.i_activation("act", [ZS[8]], [SG1], out=SG1[:], in_=ZS[8][0:32, :], func=AF.Sigmoid)
        for g in range(2):
            gc = slice(g * 128, (g + 1) * 128)
            r_s, k_s, v_s = ZS[g], ZS[2 + g], ZS[4 + g]
            G = Gb[g]
            pcg = lambda j: pc[:, g, j:j + 1]
            pw = misc()
            P.i_matmul("pe", [wdec, TW], [pw], pw[:], lhsT=wdec[0:64, gc], rhs=TW[0:64, :], start=True, stop=True)
            SIG = tmp()
            P.i_activation("act", [pw, pc], [SIG], out=SIG[:], in_=pw[:], func=AF.Sigmoid, bias=pcg(0))
            pa = misc()
            P.i_matmul("pe", [waup, ZS[6]], [pa], pa[:], lhsT=waup[64:128, gc], rhs=ZS[6][64:128, :], start=True, stop=True)
            A = tmp()
            P.i_activation("act", [pa, pc], [A], out=A[:], in_=pa[:], func=AF.Sigmoid, bias=pcg(1))
            CS = tmp()
            P.i_tensor_tensor_scan("dve", [SM, SIG], [CS], out=CS[:], data0=SM[:], data1=SIG[:], initial=0.0, op0=ALU.mult, op1=ALU.add)
            CSX = tmp()
            P.i_tensor_tensor("dve", [CS, SIG], [CSX], out=CSX[:], in0=CS[:], in1=SIG[:], op=ALU.subtract)
            GI = SIG
            P.i_activation("act", [CS], [G], out=G[:], in_=CS[:], func=AF.Exp, scale=-C0)
            GEX = CSX
            P.i_activation("act", [CSX], [GEX], out=GEX[:], in_=CSX[:], func=AF.Exp, scale=-C0)
            P.i_activation("act", [CS], [GI], out=GI[:], in_=CS[:], func=AF.Exp, scale=C0)
            KKR = tmp()
            P.i_tensor_scalar("dve", [k_s, pc], [KKR], out=KKR[:], in0=k_s[:], scalar1=pcg(2), scalar2=None, op0=ALU.mult)
            SQ = CS
            P.i_activation("act", [KKR], [SQ], out=SQ[:], in_=KKR[:], func=AF.Square)
            pss = misc()
            P.i_matmul("pe", [BD, SQ], [pss], pss[:], lhsT=BD[:], rhs=SQ[:], start=True, stop=True)
            RN = SQ
            P.i_tensor_scalar("dve", [pss], [RN], out=RN[:], in0=pss[:], scalar1=1e-24, scalar2=None, op0=ALU.max)
            P.i_activation("act", [RN], [RN], out=RN[:], in_=RN[:], func=AF.Sqrt)
            P.i_reciprocal("dve", [RN], [RN], out=RN[:], in_=RN[:])
            KK = KKR
            P.i_tensor_tensor("dve", [KKR, RN], [KK], out=KK[:], in0=KKR[:], in1=RN[:], op=ALU.mult)
            K2 = RN
            P.i_tensor_scalar("dve", [A, pc], [K2], out=K2[:], in0=A[:], scalar1=pcg(3), scalar2=pcg(7), op0=ALU.mult, op1=ALU.add)
            P.i_tensor_tensor("dve", [K2, k_s], [K2], out=K2[:], in0=K2[:], in1=k_s[:], op=ALU.mult)
            RK = tmp()
            P.i_scalar_tensor_tensor("dve", [r_s, K2, pc], [RK], out=RK[:], in0=r_s[:], scalar=pcg(4), in1=K2[:], op0=ALU.mult, op1=ALU.mult)
            prk = misc()
            P.i_matmul("pe", [BD, RK], [prk], prk[:], lhsT=BD[:], rhs=RK[:], start=True, stop=True)
            P.i_tensor_tensor("dve", [prk, v_s], [BON[g]], out=BON[g][:], in0=prk[:], in1=v_s[:], op=ALU.mult)
            pg = misc()
            P.i_matmul("pe", [wg0, SG0], [pg], pg[:], lhsT=wg0[:, gc], rhs=SG0[:], start=True, stop=False)
            P.i_matmul("pe", [wg1, SG1], [pg], pg[:], lhsT=wg1[0:32, gc], rhs=SG1[0:32, :], start=False, stop=True)
            P.i_copy("act", [pg], [GATE[g]], out=GATE[g][:], in_=pg[:])
            if stage < 2:
                continue
            KA = RK
            P.i_tensor_tensor("dve", [KK, A], [KA], out=KA[:], in0=KK[:], in1=A[:], op=ALU.mult)
            v3 = lambda ap: ap.rearrange("p (c l) -> p c l", l=64)
            P.i_tensor_tensor("dve", [KA, GI], [BK[g]], out=BK[g][:, :, 0, :], in0=v3(KA[:]), in1=v3(GI[:]), op=ALU.mult)
            P.i_tensor_tensor("dve", [K2, GI], [BK[g]], out=BK[g][:, :, 1, :], in0=v3(K2[:]), in1=v3(GI[:]), op=ALU.mult)
            P.i_scalar_tensor_tensor("dve", [KK, GEX], [AR[g]], out=AR[g][:, :, 0, :], in0=v3(KK[:]), scalar=-1.0, in1=v3(GEX[:]), op0=ALU.mult, op1=ALU.mult)
            P.i_tensor_tensor("dve", [r_s, G], [AR[g]], out=AR[g][:, :, 1, :], in0=v3(r_s[:]), in1=v3(G[:]), op=ALU.mult)
            GLb = v3(G[:])[:, :, 63:64].broadcast_to([128, 8, 64])
            P.i_tensor_tensor("dve", [BK[g], G], [BH[g]], out=BH[g][:], in0=BK[g][:, :, 0, :], in1=GLb, op=ALU.mult)
            P.i_tensor_tensor("dve", [BK[g], G], [KH[g]], out=KH[g][:], in0=BK[g][:, :, 1, :], in1=GLb, op=ALU.mult)
            P.i_copy("act", [v_s], [VB[g]], out=VB[g][:], in_=v3(v_s[:]))
            if stage < 3:
                continue
            srcs = [(AR[g], lambda c: AR[g][:, c, 0, :]), (BH[g], lambda c: BH[g][:, c, :]), (KH[g], lambda c: KH[g][:, c, :]), (VB[g], lambda c: VB[g][:, c, :])]
            for pair in range(2):
                PT = B[4].ap.bitcast(BF16).rearrange("p (a c l) -> p a c l", a=2, c=8)
                for a2 in range(2):
                    sb, fn = srcs[pair * 2 + a2]
                    for c in range(8):
                        src = fn(c)
                        for hh in (0, 64):
                            P.i_transpose("pe", [sb, ident], [B[4]], PT[hh:hh + 64, a2, c, :], src[hh:hh + 64, :], ident[hh:hh + 64, hh:hh + 64])
                for a2 in range(2):
                    dst = TM[g][pair * 2 + a2]
                    P.i_copy("act", [B[4]], [dst], out=dst[:], in_=PT[:, a2, :, :])
            if stage < 4:
                continue
            for b in range(2):
                PM1a = B[5].ap.rearrange("p (c n) -> p c n", c=4)
                PM1b = B[6].ap.rearrange("p (c n) -> p c n", c=4)
                PNT = B[2].ap[:, 0:256].rearrange("p (c n) -> p c n", c=4)
                PX = B[3].ap[:, 0:256].rearrange("p (c n) -> p c n", c=4)
                PXT = B[4].ap[:, 0:256].rearrange("p (c n) -> p c n", c=4)
                PTU = B[4].ap[:, 256:512].rearrange("p (c n) -> p c n", c=4)
                for cc in range(4):
                    c = b * 4 + cc
                    for hh in (0, 64):
                        hs = slice(hh, hh + 64)
                        P.i_matmul("pe", [BK[g], AR[g]], [B[5]], PM1a[hs, cc, :], lhsT=BK[g][hs, c, 0, :], rhs=AR[g][hs, c, :, :].rearrange("p a l -> p (a l)"), start=True, stop=True)
                        P.i_matmul("pe", [BK[g], AR[g]], [B[6]], PM1b[hs, cc, :], lhsT=BK[g][hs, c, 1, :], rhs=AR[g][hs, c, :, :].rearrange("p a l -> p (a l)"), start=True, stop=True)
                        P.i_matmul("pe", [BK[g], AR[g]], [B[2]], PNT[hs, cc, :], lhsT=AR[g][hs, c, 0, :], rhs=BK[g][hs, c, 0, :], start=True, stop=True)
                MUb = MASKU[:].rearrange("p (o n) -> p o n", o=1).broadcast_to([128, 4, 128])
                MLb = MASKL[:].rearrange("p (o n) -> p o n", o=1).broadcast_to([128, 4, 64])
                I64b = I64[:].rearrange("p (o n) -> p o n", o=1).broadcast_to([128, 4, 64])
                sa, sbb, tt = M1Sa[g][b], M1Sb[g][b], TT[g][b]
                P.i_tensor_tensor("dve", [B[5], MASKU], [sa], out=sa[:], in0=PM1a, in1=MUb, op=ALU.mult)
                P.i_tensor_tensor("dve", [B[6], MASKU], [sbb], out=sbb[:], in0=PM1b, in1=MUb, op=ALU.mult)
                X, XT = Xb[0], XTb[0]
                P.i_tensor_tensor("dve", [B[2], MASKL], [XT], out=XT[:], in0=PNT, in1=MLb, op=ALU.mult)
                P.i_tensor_copy("dve", [sa], [X], out=X[:], in_=sa[:, :, 0:64])
                P.i_tensor_tensor("dve", [sa, I64], [tt], out=tt[:], in0=sa[:, :, 0:64], in1=I64b, op=ALU.add)
                for lvl in range(1, 6):
                    Xn, XTn = Xb[lvl % 2], XTb[lvl % 2]
                    for cc in range(4):
                        for hh in (0, 64):
                            hs = slice(hh, hh + 64)
                            if lvl < 5:
                                P.i_matmul("pe", [X, XT], [B[3]], PX[hs, cc, :], lhsT=XT[hs, cc, :], rhs=X[hs, cc, :], start=True, stop=True)
                            P.i_matmul("pe", [X, XT], [B[4]], PXT[hs, cc, :], lhsT=X[hs, cc, :], rhs=XT[hs, cc, :], start=True, stop=True)
                    if lvl < 5:
                        P.i_copy("act", [B[3]], [Xn], out=Xn[:], in_=PX)
                    P.i_tensor_copy("dve", [B[4]], [XTn], out=XTn[:], in_=PXT)
                    for cc in range(4):
                        for hh in (0, 64):
                            hs = slice(hh, hh + 64)
                            P.i_matmul("pe", [XTn, tt], [B[4]], PTU[hs, cc, :], lhsT=XTn[hs, cc, :], rhs=tt[hs, cc, :], start=True, stop=True)
                    P.i_tensor_tensor("dve", [B[4], tt], [tt], out=tt[:], in0=PTU, in1=tt[:], op=ALU.add)
                    X, XT = Xn, XTn
        if stage < 5 or ti < int(os.environ.get('SERIAL_FROM', '0')):
            continue
        for c in range(8):
            for g in range(2):
                b, cc = c // 4, c % 4
                ATM, BHTM, KHTM, VTM = TM[g]
                sa, sbb, tt = M1Sa[g][b], M1Sb[g][b], TT[g][b]
                k = g
                w1s, ahs, us = W1S[k], AHS[k], US[k]
                bW1 = bAH = (B[7] if g == 0 else B[4])
                bU = bS = (B[5] if g == 0 else B[6])
                PW1, PAH = bW1.ap[:, 0:64], bW1.ap[:, 64:128]
                PU, PSn = bU.ap[:, 0:64], bU.ap[:, 64:128]
                PY = B[g]
                for hh in (0, 64):
                    hs = slice(hh, hh + 64)
                    P.i_matmul("pe", [sbb, VTM], [bW1], PW1[hs, :], lhsT=sbb[hs, cc, 0:64], rhs=VTM[hs, c, :], start=True, stop=True)
                    P.i_matmul("pe", [ATM, tt], [bAH], PAH[hs, :], lhsT=ATM[hs, c, :], rhs=tt[hs, cc, :], start=True, stop=True)
                if stage < 5.2:
                    continue
                P.i_tensor_copy("dve", [bW1], [w1s], out=w1s[:], in_=PW1)
                P.i_tensor_copy("dve", [bAH], [ahs], out=ahs[:], in_=PAH)
                for hh in (0, 64):
                    hs = slice(hh, hh + 64)
                    P.i_matmul("pe", [tt, w1s], [bU], PU[hs, :], lhsT=tt[hs, cc, :], rhs=w1s[hs, :], start=True, stop=False)
                    P.i_matmul("pe", [ahs, STB[g]], [bU], PU[hs, :], lhsT=ahs[hs, :], rhs=STB[g][hs, :], start=False, stop=True)
                if stage < 5.4:
                    continue
                P.i_tensor_copy("dve", [bU], [us], out=us[:], in_=PU)
                if stage < 5.5:
                    continue
                for hh in (0, 64):
                    hs = slice(hh, hh + 64)
                    ycol = slice(c * 64, (c + 1) * 64)
                    ysel = int(os.environ.get("YSEL", "7"))
                    ymm = [m_ for m_ in range(3) if ysel & (1 << m_)]
                    yops = [([STB[g], AR[g]], STB[g][hs, :], AR[g][hs, c, 1, :]), ([us, sa], us[hs, :], sa[hs, cc, 64:128]), ([VTM, sbb], VTM[hs, c, :], sbb[hs, cc, 64:128])]
                    for m_ in ymm:
                        P.i_matmul("pe", yops[m_][0], [PY], PY[hs, ycol], lhsT=yops[m_][1], rhs=yops[m_][2], start=(m_ == ymm[0]), stop=(m_ == ymm[-1]))
                if stage < 5.6:
                    continue
                for hh in (0, 64):
                    hs = slice(hh, hh + 64)
                    P.i_matmul("pe", [BHTM, us], [bS], PSn[hs, :], lhsT=BHTM[hs, c, :], rhs=us[hs, :], start=True, stop=False)
                    P.i_matmul("pe", [KHTM, VTM], [bS], PSn[hs, :], lhsT=KHTM[hs, c, :], rhs=VTM[hs, c, :], start=False, stop=True)
                if stage < 5.8:
                    continue
                GLc = Gb[g][:, c * 64 + 63:c * 64 + 64]
                P.i_tensor_scalar("dve", [STF[g], Gb[g]], [STF[g]], out=STF[g][:], in0=STF[g][:], scalar1=GLc, scalar2=None, op0=ALU.mult)
                if stage < 5.85:
                    continue
                P.i_tensor_tensor("dve", [STF[g], bS], [STF[g]], out=STF[g][:], in0=PSn, in1=STF[g][:], op=ALU.add)
                if stage < 5.9:
                    continue
                P.i_tensor_copy("dve", [STF[g]], [STB[g]], out=STB[g][:], in_=STF[g][:])
        if stage < 6:
            continue
        for g in range(2):
            PY = B[g]
            pcg = lambda j: pc[:, g, j:j + 1]
            P.i_copy("act", [PY], [YS], out=YS[:], in_=PY[:])
            pm = misc()
            P.i_matmul("pe", [BDm, YS], [pm], pm[:], lhsT=BDm[:], rhs=YS[:], start=True, stop=True)
            YC = tmp()
            P.i_tensor_tensor("dve", [YS, pm], [YC], out=YC[:], in0=YS[:], in1=pm[:], op=ALU.subtract)
            SQ = tmp()
            P.i_activation("act", [YC], [SQ], out=SQ[:], in_=YC[:], func=AF.Square)
            pv = misc()
            P.i_matmul("pe", [BDm, SQ], [pv], pv[:], lhsT=BDm[:], rhs=SQ[:], start=True, stop=True)
            RS = SQ
            P.i_activation("act", [pv, eps_lnx], [RS], out=RS[:], in_=pv[:], func=AF.Sqrt, bias=eps_lnx[:])
            P.i_reciprocal("dve", [RS], [RS], out=RS[:], in_=RS[:])
            P.i_tensor_tensor("dve", [YC, RS], [YC], out=YC[:], in0=YC[:], in1=RS[:], op=ALU.mult)
            P.i_tensor_scalar("dve", [YC, pc], [YC], out=YC[:], in0=YC[:], scalar1=pcg(5), scalar2=pcg(6), op0=ALU.mult, op1=ALU.add)
            P.i_tensor_tensor("dve", [YC, BON[g]], [YC], out=YC[:], in0=YC[:], in1=BON[g][:], op=ALU.add)
            ob = OUTB[g]
            P.i_tensor_tensor("dve", [YC, GATE[g]], [ob], out=ob[:], in0=YC[:], in1=GATE[g][:], op=ALU.mult)
            P.dma("sp", yT_d, ob, out_ap=yT_d.ap[g * 128:(g + 1) * 128, t0:t0 + 512], sem_buf=ob)

    if dbg_d is not None:
        for g in range(2):
            P.dma("sp", dbg_d, STF[g], out_ap=dbg_d.ap[:, g, 0:64], sem_buf=STF[g])
            P.dma("sp", dbg_d, Gb[g], out_ap=dbg_d.ap[:, g, 64:576], sem_buf=Gb[g])
            P.dma("sp", dbg_d, BON[g], out_ap=dbg_d.ap[:, g, 576:1088], sem_buf=BON[g])
        P.finish_wait("sp", [dbg_d])

    if st_out_d is not None:
        for g in range(2):
            P.dma("sp", st_out_d, STF[g], out_ap=st_out_d.ap[:, g * 64:(g + 1) * 64], sem_buf=STF[g])
        CARRYO = S([128, 72], F32, "rw_CARRYO")
        for i in range(9):
            P.i_tensor_copy("dve", [Z[i]], [CARRYO], out=CARRYO[:, i * 8:(i + 1) * 8], in_=Z[i][:, 512:520])
        P.dma("sp", st_out_d, CARRYO, out_ap=st_out_d.ap[:, 128:200], sem_buf=CARRYO)
        P.finish_wait("sp", [st_out_d])


def attn_phase(P, T, hT_d, w_d, cos_d, sin_d, rt_d, lamv_d, sg_d, yT_d, li_d):
    S = P.sbuf
    NTL = T // 512
    NB = T // 128
    W = S([128, KC, 768], BF16, "at_W")
    for c in range(KC):
        P.dma("pool", W, w_d, out_ap=W[:, c, :], in_ap=w_d.ap[c * 128:(c + 1) * 128, :])
    RT = S([128, 128], F32, "at_RT"); P.dma("sp", RT, rt_d)
    lamv = S([128, 4, 64], F32, "at_lamv"); P.dma("sp", lamv, lamv_d)
    SG = S([128, 128], F32, "at_SG"); P.dma("sp", SG, sg_d)
    li = S([128, 2], F32, "at_li"); P.dma("sp", li, li_d)
    P.i_tensor_scalar("dve", [SG, li], [SG], out=SG[:], in0=SG[:], scalar1=li[:, 1:2], scalar2=None, op0=ALU.mult)
    lp = S([128, 2, 64], F32, "at_lp")
    P.i_tensor_tensor("dve", [lamv], [lp], out=lp[:, 0, :], in0=lamv[:, 0, :], in1=lamv[:, 1, :], op=ALU.mult)
    P.i_tensor_tensor("dve", [lamv], [lp], out=lp[:, 1, :], in0=lamv[:, 2, :], in1=lamv[:, 3, :], op=ALU.mult)
    ls = S([128, 2], F32, "at_ls")
    P.i_tensor_reduce("dve", [lp], [ls], out=ls[:], in_=lp[:], axis=AX.X, op=ALU.add)
    P.i_activation("act", [ls], [ls], out=ls[:], in_=ls[:], func=AF.Exp)
    nlam = S([128, 1], F32, "at_nlam")
    P.i_tensor_tensor("dve", [ls], [nlam], out=nlam[:], in0=ls[:, 1:2], in1=ls[:, 0:1], op=ALU.subtract)
    P.i_tensor_scalar("dve", [nlam, li], [nlam], out=nlam[:], in0=nlam[:], scalar1=li[:, 0:1], scalar2=None, op0=ALU.add)
    eps = S([128, 1], F32, "at_eps"); P.i_memset("pool", [], [eps], eps[:], 1e-5)
    identf = S([128, 128], F32, "at_identf")
    P.i_memset("pool", [], [identf], identf[:], 1.0)
    P.i_affine_select("pool", [identf], [identf], out=identf[:], in_=identf[:], pattern=[[-1, 128]], compare_op=ALU.is_equal, fill=0.0, base=0, channel_multiplier=1)
    ident = S([128, 128], BF16, "at_ident")
    P.i_tensor_copy("pool", [identf], [ident], out=ident[:], in_=identf[:])
    CM = S([128, 4, 512], BF16, "at_CM")
    P.i_memset("pool", [], [CM], CM[:], 1.0)
    for j in range(4):
        P.i_affine_select("pool", [CM], [CM], out=CM[:, j, :], in_=CM[:, j, :], pattern=[[1, 512]], compare_op=ALU.is_ge, fill=0.0, base=-128 * j, channel_multiplier=-1)
    KT = S([128, 2, T], BF16, "at_KT")
    VT = S([128, NB, 2, 132], BF16, "at_VT")
    P.i_memset("pool", [], [VT], VT[:, :, :, 128:129], 1.0)
    HT = [S([128, KC, 512], BF16, "at_HT%d" % i) for i in range(2)]
    QT = [S([128, 2, 512], BF16, "at_QT%d" % i) for i in range(2)]
    XF = [S([128, 512], F32, "at_XF%d" % i) for i in range(2)]
    XC = [S([128, 512], F32, "at_XC%d" % i) for i in range(2)]
    COS = [S([128, 512], F32, "at_COS%d" % i) for i in range(2)]
    SIN = [S([128, 512], F32, "at_SIN%d" % i) for i in range(2)]
    PTs = [S([128, 512], BF16, "at_PT%d" % i) for i in range(3)]
    O1 = S([128, 4, 128], F32, "at_O1")
    O2 = S([128, 4, 128], F32, "at_O2")
    OSQ = S([128, 4, 128], F32, "at_OSQ")
    RD = S([128, 4], F32, "at_RD")
    SSQ = S([128, 4], F32, "at_SSQ")
    YB = S([128, 4, 128], BF16, "at_YB")
    YTB = [S([128, 512], BF16, "at_YTB%d" % i) for i in range(2)]
    B = [P.psum([128, 512], F32, "at_ps%d" % i) for i in range(7)]
    cnt = {"x": 0, "s": 0, "p": 0}

    for ti in range(NTL):
        t0 = ti * 512
        ht = HT[ti % 2]
        P.dma("sp", ht, hT_d, in_ap=hT_d.ap[:, t0:t0 + 512].rearrange("(c p) t -> p c t", p=128))
        cs_, sn_ = COS[ti % 2], SIN[ti % 2]
        P.dma("sp", cs_, cos_d, in_ap=cos_d.ap[:, t0:t0 + 512])
        P.dma("sp", sn_, sin_d, in_ap=sin_d.ap[:, t0:t0 + 512])
        qt = QT[ti % 2]
        for i in range(4):
            ps = B[cnt["x"] % 2]; xf = XF[cnt["x"] % 2]; xc = XC[cnt["x"] % 2]; cnt["x"] += 1
            for c in range(KC):
                P.i_matmul("pe", [W, ht], [ps], ps[:], lhsT=W[:, c, i * 128:(i + 1) * 128], rhs=ht[:, c, :], start=(c == 0), stop=(c == KC - 1))
            P.i_copy("act", [ps], [xf], out=xf[:], in_=ps[:])
            P.i_matmul("pe", [RT, xf], [ps], ps[:], lhsT=RT[:], rhs=xf[:], start=True, stop=True)
            P.i_tensor_tensor("dve", [xf, cs_], [xc], out=xc[:], in0=xf[:], in1=cs_[:], op=ALU.mult)
            P.i_tensor_tensor("dve", [ps, sn_], [xf], out=xf[:], in0=ps[:], in1=sn_[:], op=ALU.mult)
            dst = qt[:, i, :] if i < 2 else KT[:, i - 2, t0:t0 + 512]
            P.i_tensor_tensor("dve", [xf, xc], [qt if i < 2 else KT], out=dst, in0=xf[:], in1=xc[:], op=ALU.add)
        for sb in range(4):
            ps = B[cnt["x"] % 2]; cnt["x"] += 1
            for c in range(KC):
                P.i_matmul("pe", [W, ht], [ps], ps[:, 0:256], lhsT=ht[:, c, sb * 128:(sb + 1) * 128], rhs=W[:, c, 512:768], start=(c == 0), stop=(c == KC - 1))
            P.i_copy("act", [ps], [VT], out=VT[:, ti * 4 + sb, :, 0:128], in_=ps[:, 0:256].rearrange("p (h d) -> p h d", h=2))
        nkb = 4 * (ti + 1)
        for h in range(2):
            for m in range(2):
                ms = slice(m * 64, (m + 1) * 64)
                ACC = [B[4].ap[:, 0:258].rearrange("p (a d) -> p a d", a=2), B[5].ap[:, 0:258].rearrange("p (a d) -> p a d", a=2)]
                for kb in range(nkb):
                    ps = B[2 + cnt["s"] % 2]; cnt["s"] += 1
                    P.i_matmul("pe", [KT, qt], [ps], ps[:], lhsT=KT[ms, h, kb * 128:(kb + 1) * 128], rhs=qt[ms, h, :], start=True, stop=True)
                    pt = PTs[cnt["p"] % 3]; cnt["p"] += 1
                    P.i_activation("act", [ps], [pt], out=pt[:], in_=ps[:], func=AF.Exp, scale=0.125)
                    j = kb - 4 * ti
                    if j >= 0:
                        P.i_tensor_tensor("dve", [pt, CM], [pt], out=pt[:], in0=pt[:], in1=CM[:, j, :], op=ALU.mult)
                    for qs in range(4):
                        if j > qs:
                            continue
                        bank = qs // 2
                        first = (kb == 0 and qs % 2 == 0)
                        last = (kb == min(nkb - 1, 4 * ti + qs))
                        P.i_matmul("pe", [pt, VT], [B[4 + bank]], ACC[bank][:, qs % 2, :], lhsT=pt[:, qs * 128:(qs + 1) * 128], rhs=VT[:, kb, h, 0:129], start=first, stop=last, skip_group_check=True)
                for bank in range(2):
                    qsl = slice(bank * 2, bank * 2 + 2)
                    P.i_reciprocal("dve", [B[4 + bank]], [RD], out=RD[:, qsl], in_=ACC[bank][:, :, 128])
                    dstO = O1 if m == 0 else O2
                    P.i_tensor_tensor("dve", [B[4 + bank], RD], [dstO], out=dstO[:, qsl, :], in0=ACC[bank][:, :, 0:128], in1=RD[:, qsl].rearrange("p (a o) -> p a o", o=1).broadcast_to([128, 2, 128]), op=ALU.mult)
            P.i_scalar_tensor_tensor("dve", [O2, nlam, O1], [O1], out=O1[:], in0=O2[:], scalar=nlam[:], in1=O1[:], op0=ALU.mult, op1=ALU.add)
            P.i_activation("act", [O1], [OSQ], out=OSQ[:], in_=O1[:], func=AF.Square)
            P.i_tensor_reduce("dve", [OSQ], [SSQ], out=SSQ[:], in_=OSQ[:], axis=AX.X, op=ALU.add)
            P.i_activation("act", [SSQ, eps], [SSQ], out=SSQ[:], in_=SSQ[:], func=AF.Sqrt, scale=1.0 / 128, bias=eps[:])
            P.i_reciprocal("dve", [SSQ], [SSQ], out=SSQ[:], in_=SSQ[:])
            P.i_tensor_tensor("dve", [O1, SSQ], [O1], out=O1[:], in0=O1[:], in1=SSQ[:].rearrange("p (a o) -> p a o", o=1).broadcast_to([128, 4, 128]), op=ALU.mult)
            P.i_tensor_tensor("dve", [O1, SG], [YB], out=YB[:], in0=O1[:], in1=SG[:].rearrange("p (o d) -> p o d", o=1).broadcast_to([128, 4, 128]), op=ALU.mult)
            PTR = B[6].ap.bitcast(BF16)[:, 0:512]
            for qs in range(4):
                P.i_transpose("pe", [YB, ident], [B[6]], PTR[:, qs * 128:(qs + 1) * 128], YB[:, qs, :], ident[:])
            yb = YTB[h]
            P.i_copy("act", [B[6]], [yb], out=yb[:], in_=PTR)
            P.dma("sp", yT_d, yb, out_ap=yT_d.ap[h * 128:(h + 1) * 128, t0:t0 + 512], sem_buf=yb)
import ml_dtypes
NTC = 2048
FH = 5632
SEQ = 8192
RSEG = 8192


def proj_residual(dn, xT_d, t0, ntok, w_d, kch, rhs, rhs_buf):
    P = dn.P
    for dc in range(KC):
        db, dv = dn.load_w(w_d, w_d.ap[:, dc * 128:(dc + 1) * 128].rearrange("(c p) n -> p c n", p=128), (kch, 128))
        xr = dn.xres[dc % 2]
        P.dma("sp", xr, xT_d, out_ap=xr[:, 0:ntok], in_ap=xT_d.ap[dc * 128:(dc + 1) * 128, t0:t0 + ntok])
        for s in range(ntok // 512):
            pd = dn.next_ps()
            for c in range(kch):
                P.i_matmul("pe", [db, rhs_buf], [pd], pd[:], lhsT=dv[:, c, :], rhs=rhs[:, c, s * 512:(s + 1) * 512], start=(c == 0), stop=(c == kch - 1))
            P.i_tensor_tensor("dve", [pd, xr], [xr], out=xr[:, s * 512:(s + 1) * 512], in0=pd[:], in1=xr[:, s * 512:(s + 1) * 512], op=ALU.add)
        P.dma("sp", xT_d, xr, out_ap=xT_d.ap[dc * 128:(dc + 1) * 128, t0:t0 + ntok], in_ap=xr[:, 0:ntok], sem_buf=xr)


class GMLP:
    def __init__(self, dn):
        P = dn.P
        self.dn = dn
        self.vtm = P.sbuf([128, 4, 2048], BF16, "gm_vtm")
        self.gt = [P.sbuf([128, 256], F32, "gm_gt%d" % i) for i in range(2)]
        self.gti = 0
        self.ST = P.sbuf([128, 4, 8, 6], F32, "gm_ST")
        self.MV = P.sbuf([128, 4, 2], F32, "gm_MV")
        self.RS = P.sbuf([128, 4], F32, "gm_RS")
        self.eps = P.sbuf([128, 1], F32, "gm_eps")
        P.i_memset("pool", [], [self.eps], self.eps[:], 1e-5)
        self.wsT = P.sbuf([128, 16, 128], BF16, "gm_wsT")
        self.T2 = P.sbuf([128, 16, 128], F32, "gm_T2")
        self.tmp = [P.sbuf([128, 4, 128], F32, "gm_tmp%d" % i) for i in range(2)]
        self.tmpi = 0
        self.lng = P.sbuf([128, 16], F32, "gm_lng")
        self.lnb = P.sbuf([128, 16], F32, "gm_lnb")

    def setup(self, wsT_d, bsb_d, lng_d, lnb_d):
        dn, P = self.dn, self.dn.P
        P.dma("sp", self.lng, lng_d)
        P.dma("sp", self.lnb, lnb_d)
        P.dma("sp", self.T2, bsb_d)
        stg = dn.xst[0]
        sv = stg.ap[:, :, :]
        P.dma("sp", stg, wsT_d, out_ap=sv)
        P.i_affine_select("pool", [stg], [stg], out=sv, in_=sv, pattern=[[0, 16], [1, 128]], compare_op=ALU.is_ge, fill=0.0, base=0, channel_multiplier=-1)
        P.i_tensor_copy("pool", [stg], [self.wsT], out=self.wsT[:], in_=sv)
        for g4 in range(4):
            ps = dn.next_ps()
            for gg in range(4):
                g = g4 * 4 + gg
                P.i_matmul("pe", [dn.ones_bf, self.wsT], [ps], ps[:, gg * 128:(gg + 1) * 128], lhsT=dn.ones_bf[:], rhs=self.wsT[:, g, :], start=True, stop=True)
            for gg in range(4):
                g = g4 * 4 + gg
                P.i_scalar_tensor_tensor("dve", [ps, self.lnb, self.T2], [self.T2], out=self.T2[:, g, :], in0=ps[:, gg * 128:(gg + 1) * 128], scalar=self.lnb[:, g:g + 1], in1=self.T2[:, g, :], op0=ALU.mult, op1=ALU.add)

    def run(self, xT_d, t0, win_d, wout_d):
        dn, P = self.dn, self.dn.P
        hT, aT = dn.hT, dn.aT
        uT = aT.ap[:, 0:16, 0:512]
        vtm = self.vtm
        for fb in range(0, 16, 2):
            wb_, wv = dn.load_w(win_d, win_d.ap[:, fb * 128:(fb + 2) * 128].rearrange("(c p) n -> p c n", p=128), (KC, 256))
            for jj in range(2):
                ps = dn.next_ps()
                for c in range(KC):
                    P.i_matmul("pe", [wb_, hT], [ps], ps[:], lhsT=wv[:, c, jj * 128:(jj + 1) * 128], rhs=hT[:, c, 0:512], start=(c == 0), stop=(c == KC - 1))
                P.i_activation("act", [ps], [aT], out=uT[:, fb + jj, :], in_=ps[:], func=AF.Gelu)
        for cb in range(8):
            wb_, wv = dn.load_w(win_d, win_d.ap[:, 2048 + cb * 256:2048 + (cb + 1) * 256].rearrange("(c p) n -> p c n", p=128), (KC, 256))
            for tc in range(4):
                ps = dn.next_ps()
                for c in range(KC):
                    P.i_matmul("pe", [wb_, hT], [ps], ps[:, 0:256], lhsT=hT[:, c, tc * 128:(tc + 1) * 128], rhs=wv[:, c, :], start=(c == 0), stop=(c == KC - 1))
                gt = self.gt[self.gti % 2]
                self.gti += 1
                P.i_activation("act", [ps], [gt], out=gt[:], in_=ps[:, 0:256], func=AF.Gelu)
                P.op("dve", lambda e, gt=gt, tc=tc, cb=cb: e.bn_stats(out=self.ST[:, tc, cb, :], in_=gt[:]), [gt], [self.ST])
                P.i_tensor_copy("pool", [gt], [vtm], out=vtm[:, tc, cb * 256:(cb + 1) * 256], in_=gt[:])
        for tc in range(4):
            P.op("dve", lambda e, tc=tc: e.bn_aggr(out=self.MV[:, tc, :], in_=self.ST[:, tc, :, :].rearrange("p a b -> p (a b)")), [self.ST], [self.MV])
        P.i_activation("act", [self.MV, self.eps], [self.RS], out=self.RS[:], in_=self.MV[:, :, 1], func=AF.Sqrt, bias=self.eps[:])
        P.i_reciprocal("dve", [self.RS], [self.RS], out=self.RS[:], in_=self.RS[:])
        for tc in range(4):
            P.i_tensor_scalar("dve", [vtm, self.MV, self.RS], [vtm], out=vtm[:, tc, :], in0=vtm[:, tc, :], scalar1=self.MV[:, tc, 0:1], scalar2=self.RS[:, tc:tc + 1], op0=ALU.subtract, op1=ALU.mult)
        for tc in range(4):
            for g4 in range(4):
                ps = dn.next_ps()
                for gg in range(4):
                    g = g4 * 4 + gg
                    P.i_matmul("pe", [vtm, self.wsT], [ps], ps[:, gg * 128:(gg + 1) * 128], lhsT=vtm[:, tc, g * 128:(g + 1) * 128], rhs=self.wsT[:, g, :], start=True, stop=True)
                tmp = self.tmp[self.tmpi % 2]
                self.tmpi += 1
                for gg in range(4):
                    g = g4 * 4 + gg
                    P.i_scalar_tensor_tensor("dve", [ps, self.lng, self.T2], [tmp], out=tmp[:, gg, :], in0=ps[:, gg * 128:(gg + 1) * 128], scalar=self.lng[:, g:g + 1], in1=self.T2[:, g, :], op0=ALU.mult, op1=ALU.add)
                usl = uT[:, g4 * 4:(g4 + 1) * 4, tc * 128:(tc + 1) * 128]
                P.i_tensor_tensor("dve", [tmp, aT], [aT], out=usl, in0=usl, in1=tmp[:], op=ALU.mult)
        proj_residual(dn, xT_d, t0, 512, wout_d, 16, uT, aT)


def _inp(nc, n, s, d=F32):
    return nc.dram_tensor(n, list(s), d, kind="ExternalInput").ap()


def build_e1():
    nc = bass.Bass("TRN2", target_bir_lowering=False)
    x = _inp(nc, "xT", [D, NTC]); gn = _inp(nc, "gn", [128, KC])
    h = nc.dram_tensor("hT_out", [D, NTC], BF16, kind="ExternalOutput").ap()
    P = Prog(nc)
    xd, hd = P.view(x), P.view(h)
    dn = Dense(P, NTC)
    g_t = P.sbuf([128, KC], F32, "g_t"); P.dma("sp", g_t, P.view(gn))
    for t0 in range(0, NTC, dn.TT):
        dn.rmsnorm(xd, t0, g_t)
        P.dma("sp", hd, dn.hT, out_ap=h[:, t0:t0 + dn.TT].rearrange("(c p) t -> p c t", p=128), sem_buf=dn.hT)
    P.finish_wait("sp", [hd])
    P.build()
    return nc


def build_rwkv(T):
    nc = bass.Bass("TRN2", target_bir_lowering=False)
    hT = _inp(nc, "hT_in", [2048, T], BF16)
    w = _inp(nc, "w", [2048, NCOL]); mu = _inp(nc, "mu", [128, 9]); pc = _inp(nc, "pc", [128, 2, 8])
    wdec = _inp(nc, "wdec", [64, 256]); waup = _inp(nc, "waup", [64, 256]); wgup = _inp(nc, "wgup", [160, 256])
    use_state = (T != SEQ)
    if use_state:
        st_in = _inp(nc, "st_in", [128, 200])
    ya = nc.dram_tensor("yaT", [256, T], BF16, kind="ExternalOutput").ap()
    if use_state:
        st_out = nc.dram_tensor("st_out", [128, 200], F32, kind="ExternalOutput").ap()
    P = Prog(nc)
    yad = P.view(ya)
    rwkv_phase(P, T, P.view(hT), P.view(w), P.view(mu), P.view(pc), P.view(wdec), P.view(waup), P.view(wgup), yad,
               st_in_d=(P.view(st_in) if use_state else None), st_out_d=(P.view(st_out) if use_state else None))
    P.finish_wait("sp", [yad])
    P.build()
    return nc


def build_attn(T):
    nc = bass.Bass("TRN2", target_bir_lowering=False)
    hT = _inp(nc, "hT_in", [2048, T], BF16)
    wq = _inp(nc, "wq", [2048, 768]); cos = _inp(nc, "cos", [128, T]); sin = _inp(nc, "sin", [128, T]); rt = _inp(nc, "rt", [128, 128])
    lamv = _inp(nc, "lamv", [128, 4, 64]); sg = _inp(nc, "sg", [128, 128]); li = _inp(nc, "li", [128, 2])
    yb = nc.dram_tensor("ybT", [256, T], BF16, kind="ExternalOutput").ap()
    P = Prog(nc)
    ybd = P.view(yb)
    attn_phase(P, T, P.view(hT), P.view(wq), P.view(cos), P.view(sin), P.view(rt), P.view(lamv), P.view(sg), ybd, P.view(li))
    P.finish_wait("sp", [ybd])
    P.build()
    return nc


def build_e3(last):
    nc = bass.Bass("TRN2", target_bir_lowering=False)
    x = _inp(nc, "xT", [D, NTC]); yT = _inp(nc, "yT", [D, NTC], BF16); wo = _inp(nc, "wo", [D, D])
    gains = _inp(nc, "gains", [128, 4, KC])
    f = [dict(wg=_inp(nc, "wg%d" % k, [D, FH]), wu=_inp(nc, "wu%d" % k, [D, FH]), wd=_inp(nc, "wd%d" % k, [FH, D])) for k in range(2)]
    win = _inp(nc, "win", [D, 4096]); wout = _inp(nc, "wout", [D, D])
    wsT = _inp(nc, "wsT", [128, 16, 128]); bsb = _inp(nc, "bsb", [128, 16, 128]); lng = _inp(nc, "lng", [128, 16]); lnb = _inp(nc, "lnb", [128, 16])
    xo = nc.dram_tensor("xT_out", [D, NTC], F32, kind="ExternalOutput").ap()
    if last:
        fo = nc.dram_tensor("fin", [D, NTC], F32, kind="ExternalOutput").ap()
    else:
        fo = nc.dram_tensor("hT_out", [D, NTC], BF16, kind="ExternalOutput").ap()
    P = Prog(nc)
    xd, xod, fod, yTd = P.view(x), P.view(xo), P.view(fo), P.view(yT)
    dn = Dense(P, NTC); dn.alloc_ffn(FH)
    gm = GMLP(dn)
    g_ts = []
    for k in range(4):
        gt_ = P.sbuf([128, KC], F32, "g_t%d" % k)
        P.dma("sp", gt_, P.view(gains), in_ap=gains[:, k, :])
        g_ts.append(gt_)
    for c in range(KC):
        P.dma("sp", xod, xd, out_ap=xo[c * 128:(c + 1) * 128, :], in_ap=x[c * 128:(c + 1) * 128, :], sem_buf=dn.xres[0])
    TT = dn.TT
    for t0 in range(0, NTC, TT):
        P.dma("sp", dn.aT, yTd, out_ap=dn.aT.ap[:, 0:16, :], in_ap=yT[:, t0:t0 + TT].rearrange("(c p) t -> p c t", p=128))
        proj_residual(dn, xod, t0, TT, P.view(wo), 16, dn.aT.ap[:, 0:16, :], dn.aT)
    for t0 in range(0, NTC, TT):
        dn.rmsnorm(xod, t0, g_ts[0])
        dn.ffn(xod, t0, P.view(f[0]["wg"]), P.view(f[0]["wu"]), P.view(f[0]["wd"]), dn.aT, FH, dn.xres)
    gm.setup(P.view(wsT), P.view(bsb), P.view(lng), P.view(lnb))
    for t0 in range(0, NTC, 512):
        dn.rmsnorm(xod, t0, g_ts[1], ntok=512)
        gm.run(xod, t0, P.view(win), P.view(wout))
    for t0 in range(0, NTC, TT):
        dn.rmsnorm(xod, t0, g_ts[2])
        dn.ffn(xod, t0, P.view(f[1]["wg"]), P.view(f[1]["wu"]), P.view(f[1]["wd"]), dn.aT, FH, dn.xres)
    for t0 in range(0, NTC, TT):
        if last:
            dn.rmsnorm(xod, t0, g_ts[3], out_f32_d=fod)
        else:
            dn.rmsnorm(xod, t0, g_ts[3])
            P.dma("sp", fod, dn.hT, out_ap=fo[:, t0:t0 + TT].rearrange("(c p) t -> p c t", p=128), sem_buf=dn.hT)
    P.finish_wait("sp", [xod, fod])
    P.build()
    return nc


def _pcol(v):
    return np.ascontiguousarray(np.asarray(v, np.float32).reshape(-1, 128).T)


def _rope_consts(T):
    inv = (10000.0 ** (-np.arange(0, 64, 2, dtype=np.float32) / 64)).astype(np.float32)
    ang = np.arange(T, dtype=np.float32)[:, None] * inv[None, :]
    c, s = np.cos(ang).astype(np.float32), np.sin(ang).astype(np.float32)
    idx = np.arange(128) % 32
    cosT = np.ascontiguousarray(c[:, idx].T)
    sinT = np.ascontiguousarray(s[:, idx].T)
    RT = np.zeros((128, 128), np.float32)
    for po in range(128):
        if po % 64 < 32:
            RT[po + 32, po] = -1.0
        else:
            RT[po - 32, po] = 1.0
    return cosT, sinT, RT


_PROGS = {}


def _prog(key, fn):
    if key not in _PROGS:
        _PROGS[key] = fn()
    return _PROGS[key]


def _run(nc, in_maps):
    res = run_bass_kernel_spmd(nc, in_maps, core_ids=list(range(8)))
    return res.results


def kernel(**inp):
    inp = {k: np.asarray(v) for k, v in inp.items()}
    x = inp["x"]
    B, T, Dm = x.shape
    cores = [(c // 4, c % 4) for c in range(8)]
    C = 1024
    cosT, sinT, RT = _rope_consts(T)

    def even_layer(i, hT_c):
        j = i // 2
        li = 0.8 - 0.6 * math.exp(-0.3 * i)
        hT_b = [np.ascontiguousarray(np.concatenate([hT_c[b * 4 + q] for q in range(4)], axis=1)) for b in range(B)]
        Wfull = inp["ev_w_in"][j]
        maps = []
        for (b, g) in cores:
            chs = np.arange(g * 256, g * 256 + 256)
            cols = np.concatenate([chs, C + chs, 2 * C + chs, np.arange(3 * C, 3 * C + 288)])
            mu_c = inp["ev_mu"][j][cols]
            mu_t = np.zeros((128, 9), np.float32)
            for k in range(8):
                mu_t[:, k] = mu_c[k * 128:(k + 1) * 128]
            mu_t[:32, 8] = mu_c[1024:1056]
            per = lambda v: v.reshape(-1)[chs].reshape(2, 128).T
            pcv = np.zeros((128, 2, 8), np.float32)
            for idx, nm in enumerate(["ev_w0", "ev_a0", "ev_k_k", "ev_k_a", "ev_r_k", "ev_lnx_w", "ev_lnx_b"]):
                pcv[:, :, idx] = per(inp[nm][j])
            pcv[:, :, 7] = np.float32(1.0) - pcv[:, :, 3]
            maps.append({"w": np.ascontiguousarray(Wfull[:, cols]), "mu": mu_t, "pc": pcv,
                         "wdec": np.ascontiguousarray(inp["ev_w_dec_up"][j][:, chs]), "waup": np.ascontiguousarray(inp["ev_w_a_up"][j][:, chs]),
                         "wgup": np.ascontiguousarray(inp["ev_w_g_up"][j][:, chs])})
        state = [np.zeros((128, 200), np.float32) for _ in range(8)]
        segs = [[] for _ in range(8)]
        for k in range(T // RSEG):
            mk_ = [dict(maps[c], hT_in=np.ascontiguousarray(hT_b[cores[c][0]][:, k * RSEG:(k + 1) * RSEG])) for c in range(8)]
            if RSEG != T:
                for c in range(8):
                    mk_[c]["st_in"] = state[c]
            rr = _run(_prog(("rwkv", RSEG), lambda: build_rwkv(RSEG)), mk_)
            for c in range(8):
                if RSEG != T:
                    state[c] = rr[c]["st_out"]
                segs[c].append(rr[c]["yaT"])
        ra = [{"yaT": np.concatenate(segs[c], axis=1)} for c in range(8)]
        lamv = np.ascontiguousarray(np.broadcast_to(np.stack([inp["ev_lam_q1"][j], inp["ev_lam_k1"][j], inp["ev_lam_q2"][j], inp["ev_lam_k2"][j]])[None], (128, 4, 64))).astype(np.float32)
        sg = np.ascontiguousarray(np.broadcast_to(inp["ev_subln_g"][j][None], (128, 128))).astype(np.float32)
        liv = np.ascontiguousarray(np.broadcast_to(np.array([-li, 1.0 - li], np.float32)[None], (128, 2)))
        maps = []
        for (b, g) in cores:
            hc = np.arange(g * 256, g * 256 + 256)
            cols = 3360 + np.concatenate([hc, 1024 + hc, 2048 + hc])
            maps.append({"hT_in": hT_b[b], "wq": np.ascontiguousarray(Wfull[:, cols]), "cos": cosT, "sin": sinT, "rt": RT, "lamv": lamv, "sg": sg, "li": liv})
        rb = _run(_prog(("attn", T), lambda: build_attn(T)), maps)
        yT_c = []
        for c, (b, q) in enumerate(cores):
            ts = slice(q * NTC, (q + 1) * NTC)
            parts = [ra[b * 4 + g]["yaT"][:, ts] for g in range(4)] + [rb[b * 4 + g]["ybT"][:, ts] for g in range(4)]
            yT_c.append(np.ascontiguousarray(np.concatenate(parts, axis=0)))
        return yT_c

    def e3(i, xT_c, yT_c, last):
        j = i // 2
        jo = (i + 1) // 2
        nxt = inp["final_norm"] if last else inp["mix_norm"][i + 2]
        gains = np.ascontiguousarray(np.stack([_pcol(inp["ffn_norm"][i]), _pcol(inp["mix_norm"][i + 1]), _pcol(inp["ffn_norm"][i + 1]), _pcol(nxt)], axis=1))
        wsT = np.ascontiguousarray(inp["od_w_s"][jo].transpose(2, 0, 1))
        bsb = np.ascontiguousarray(np.broadcast_to(inp["od_b_s"][jo][None], (128, 16, 128))).astype(np.float32)
        shared = {"wo": inp["ev_w_out"][j], "gains": gains,
                  "wg0": inp["ffn_w_gate"][i], "wu0": inp["ffn_w_up"][i], "wd0": inp["ffn_w_down"][i],
                  "wg1": inp["ffn_w_gate"][i + 1], "wu1": inp["ffn_w_up"][i + 1], "wd1": inp["ffn_w_down"][i + 1],
                  "win": inp["od_w_in"][jo], "wout": inp["od_w_out"][jo], "wsT": wsT, "bsb": bsb,
                  "lng": _pcol(inp["od_ln_g"][jo]), "lnb": _pcol(inp["od_ln_b"][jo])}
        maps = [dict(shared, xT=xT_c[c], yT=yT_c[c]) for c in range(8)]
        return _run(_prog(("e3", last), lambda: build_e3(last)), maps)

    xT_c = [np.ascontiguousarray(x[b, q * NTC:(q + 1) * NTC, :].T) for (b, q) in cores]
    g0 = _pcol(inp["mix_norm"][0])
    r = _run(_prog("e1", build_e1), [{"xT": xT_c[c], "gn": g0} for c in range(8)])
    hT_c = [r[c]["hT_out"] for c in range(8)]
    yT_c = even_layer(0, hT_c)
    r = e3(0, xT_c, yT_c, False)
    xT_c = [r[c]["xT_out"] for c in range(8)]
    hT_c = [r[c]["hT_out"] for c in range(8)]
    yT_c = even_layer(2, hT_c)
    r = e3(2, xT_c, yT_c, True)
    out = np.empty((B, T, Dm), np.float32)
    for c, (b, q) in enumerate(cores):
        out[b, q * NTC:(q + 1) * NTC, :] = r[c]["fin"].T
    return out
```
